# Optimizing a Trainium2 kernel written in Bass

```python
import math
import jax, jax.numpy as jnp
from jax import lax
import numpy as np

D_MODEL = 1024
BATCH = 4
SEQ = 8192
DEPTH = 2

ATTN_HEADS = 4
HEAD_DIM = 64
ATTN_W = ATTN_HEADS * 2 * HEAD_DIM
ROT_DIM = HEAD_DIM // 4
ROPE_THETA = 500000.0
Q_BLOCK = 128
CONV_W = 512
CONV_K = 31
D_FF = ((int(math.ceil(8 * D_MODEL / 3)) + 255) // 256) * 256
N_BRANCH = 2
Q_OFF = 0
K_OFF = Q_OFF + ATTN_W
V_OFF = K_OFF + ATTN_W
C_OFF = V_OFF + ATTN_W
G_OFF = C_OFF + 2 * CONV_W
N_IN = G_OFF + N_BRANCH * D_MODEL
N_MOD = 6

kernel_name = "gated_parallel_diffattn_conformer_hybrid"


def rms_norm(x, g, eps=1e-6):
    xf = x.astype(jnp.float32)
    y = xf * lax.rsqrt(jnp.mean(xf * xf, axis=-1, keepdims=True) + eps)
    return (y * g.astype(jnp.float32)).astype(x.dtype)


def layer_norm(x, g, b, eps=1e-5):
    xf = x.astype(jnp.float32)
    mu = jnp.mean(xf, axis=-1, keepdims=True)
    var = jnp.mean(jnp.square(xf - mu), axis=-1, keepdims=True)
    y = (xf - mu) * lax.rsqrt(var + eps)
    return (y * g.astype(jnp.float32) + b.astype(jnp.float32)).astype(x.dtype)


def partial_rope(t, cos, sin):
    half = ROT_DIM // 2
    x1 = t[..., :half]
    x2 = t[..., half:ROT_DIM]
    rot = jnp.concatenate([x1 * cos - x2 * sin, x2 * cos + x1 * sin], axis=-1)
    return jnp.concatenate([rot, t[..., ROT_DIM:]], axis=-1)


def diff_attention(q, k, v, lam):
    b, s, h, _, d = q.shape
    scale = 1.0 / math.sqrt(d)
    qt = q.transpose(0, 2, 3, 1, 4)
    kt = k.transpose(0, 2, 3, 1, 4)
    vt = v.transpose(0, 2, 1, 3)
    k_pos = jnp.arange(s)
    neg = jnp.finfo(jnp.float32).min

    def block(i):
        start = i * Q_BLOCK
        qb = lax.dynamic_slice_in_dim(qt, start, Q_BLOCK, axis=3)
        sc = jnp.einsum('bhcqd,bhckd->bhcqk', qb, kt).astype(jnp.float32) * scale
        q_pos = start + jnp.arange(Q_BLOCK)
        mask = q_pos[:, None] >= k_pos[None, :]
        sc = jnp.where(mask, sc, neg)
        p = jax.nn.softmax(sc, axis=-1)
        a = p[:, :, 0] - lam * p[:, :, 1]
        return jnp.einsum('bhqk,bhkv->bhqv', a.astype(vt.dtype), vt)

    out = lax.map(block, jnp.arange(s // Q_BLOCK))
    out = out.transpose(1, 0, 3, 2, 4)
    return out.reshape(b, s, h, 2 * d)


def causal_depthwise_conv(u, w, bias):
    y = lax.conv_general_dilated(
        u, w.astype(u.dtype), window_strides=(1,), padding=[(CONV_K - 1, 0)],
        dimension_numbers=('NWC', 'WIO', 'NWC'), feature_group_count=u.shape[-1])
    return y + bias


def setup_inputs(seed: int = 0) -> dict:
    key = jax.random.key(seed)
    ks = jax.random.split(key, 24)
    f32 = jnp.float32

    def nrm(k, shape, s):
        return jax.random.normal(k, shape, f32) * s

    L = DEPTH
    return {
        "x": nrm(ks[0], (BATCH, SEQ, D_MODEL), 1.0),
        "c": nrm(ks[1], (BATCH, D_MODEL), 1.0),
        "positions": jnp.broadcast_to(jnp.arange(SEQ, dtype=jnp.int32)[None, :], (BATCH, SEQ)),
        "ada_w": nrm(ks[2], (L, D_MODEL, N_MOD * D_MODEL), 0.1 * D_MODEL ** -0.5),
        "ada_b": nrm(ks[3], (L, N_MOD * D_MODEL), 0.01),
        "norm1_g": 1.0 + nrm(ks[4], (L, D_MODEL), 0.01),
        "w_in": nrm(ks[5], (L, D_MODEL, N_IN), D_MODEL ** -0.5),
        "lambda_q1": nrm(ks[6], (L, HEAD_DIM), 0.1),
        "lambda_k1": nrm(ks[7], (L, HEAD_DIM), 0.1),
        "lambda_q2": nrm(ks[8], (L, HEAD_DIM), 0.1),
        "lambda_k2": nrm(ks[9], (L, HEAD_DIM), 0.1),
        "subln_g": 1.0 + nrm(ks[10], (L, 2 * HEAD_DIM), 0.01),
        "w_attn_o": nrm(ks[11], (L, ATTN_W, D_MODEL), ATTN_W ** -0.5),
        "dw_conv_w": nrm(ks[12], (L, CONV_K, 1, CONV_W), CONV_K ** -0.5),
        "dw_conv_b": nrm(ks[13], (L, CONV_W), 0.01),
        "conv_ln_g": 1.0 + nrm(ks[14], (L, CONV_W), 0.01),
        "conv_ln_b": nrm(ks[15], (L, CONV_W), 0.01),
        "w_conv_o": nrm(ks[16], (L, CONV_W, D_MODEL), CONV_W ** -0.5),
        "w_out": nrm(ks[17], (L, D_MODEL, D_MODEL), D_MODEL ** -0.5),
        "norm2_g": 1.0 + nrm(ks[18], (L, D_MODEL), 0.01),
        "w_ffn_in": nrm(ks[19], (L, D_MODEL, 2 * D_FF), D_MODEL ** -0.5),
        "w_ffn_out": nrm(ks[20], (L, D_FF, D_MODEL), D_FF ** -0.5),
        "final_g": 1.0 + nrm(ks[21], (D_MODEL,), 0.01),
    }


def reference(x, c, positions, ada_w, ada_b, norm1_g, w_in, lambda_q1, lambda_k1,
              lambda_q2, lambda_k2, subln_g, w_attn_o, dw_conv_w, dw_conv_b,
              conv_ln_g, conv_ln_b, w_conv_o, w_out, norm2_g, w_ffn_in, w_ffn_out,
              final_g):
    b, s, _ = x.shape
    inv_freq = 1.0 / (ROPE_THETA ** (jnp.arange(0, ROT_DIM, 2, dtype=jnp.float32) / ROT_DIM))
    ang = positions.astype(jnp.float32)[..., None] * inv_freq
    cos = jnp.cos(ang)[:, :, None, None, :].astype(x.dtype)
    sin = jnp.sin(ang)[:, :, None, None, :].astype(x.dtype)
    c_act = jax.nn.silu(c)

    for l in range(DEPTH):
        mod = c_act @ ada_w[l] + ada_b[l]
        shift1, scale1, gate1, shift2, scale2, gate2 = [
            m[:, None, :] for m in jnp.split(mod, N_MOD, axis=-1)]

        h = rms_norm(x, norm1_g[l]) * (1.0 + scale1) + shift1
        z = h @ w_in[l]

        q = z[..., Q_OFF:K_OFF].reshape(b, s, ATTN_HEADS, 2, HEAD_DIM)
        k = z[..., K_OFF:V_OFF].reshape(b, s, ATTN_HEADS, 2, HEAD_DIM)
        v = z[..., V_OFF:C_OFF].reshape(b, s, ATTN_HEADS, 2 * HEAD_DIM)
        q = partial_rope(q, cos, sin)
        k = partial_rope(k, cos, sin)
        lambda_init = 0.8 - 0.6 * math.exp(-0.3 * l)
        lam = (jnp.exp(jnp.sum(lambda_q1[l].astype(jnp.float32) * lambda_k1[l].astype(jnp.float32)))
               - jnp.exp(jnp.sum(lambda_q2[l].astype(jnp.float32) * lambda_k2[l].astype(jnp.float32)))
               + lambda_init)
        o = diff_attention(q, k, v, lam)
        o = rms_norm(o, subln_g[l], eps=1e-5) * (1.0 - lambda_init)
        y_attn = o.reshape(b, s, ATTN_W) @ w_attn_o[l]

        ga, gb = jnp.split(z[..., C_OFF:G_OFF], 2, axis=-1)
        u = ga * jax.nn.sigmoid(gb)
        u = causal_depthwise_conv(u, dw_conv_w[l], dw_conv_b[l])
        u = jax.nn.silu(layer_norm(u, conv_ln_g[l], conv_ln_b[l]))
        y_conv = u @ w_conv_o[l]

        g_attn, g_conv = jnp.split(jax.nn.sigmoid(z[..., G_OFF:]), N_BRANCH, axis=-1)
        mixed = (g_attn * y_attn + g_conv * y_conv) @ w_out[l]
        x = x + (1.0 + gate1) * mixed

        h2 = rms_norm(x, norm2_g[l]) * (1.0 + scale2) + shift2
        f_gate, f_up = jnp.split(h2 @ w_ffn_in[l], 2, axis=-1)
        x = x + (1.0 + gate2) * ((jax.nn.silu(f_gate) * f_up) @ w_ffn_out[l])

    return rms_norm(x, final_g)
```

```python
import math
from contextlib import ExitStack

import numpy as np
import ml_dtypes
import concourse.bass as bass
import concourse.mybir as mybir
from concourse.bass_utils import run_bass_kernel_spmd

F32 = mybir.dt.float32
BF16 = mybir.dt.bfloat16
I32 = mybir.dt.int32
AF = mybir.ActivationFunctionType
ALU = mybir.AluOpType
AX = mybir.AxisListType

D = 1024
DEPTH = 2
NH = 4
NIN = 4608
DFF = 2816
CK = 31
TB = 512
DEFER_TR = 1
PI = math.pi


class Buf:
    __slots__ = ("name", "lo", "hi", "w", "r", "excl")

    def __init__(self, name, lo=0, hi=0):
        self.name = name
        self.lo = lo
        self.hi = hi
        self.w = None
        self.r = []
        self.excl = False


class Op:
    __slots__ = ("eng", "fn", "deps", "signal", "idx", "dma", "sem", "val", "prev")

    def __init__(self, eng, fn, dma):
        self.eng = eng
        self.fn = fn
        self.dma = dma
        self.deps = []
        self.signal = False
        self.idx = -1
        self.sem = None
        self.val = 0
        self.prev = None


ENGS = ("pe", "act", "dve", "pool", "sp")
SEM_LIMIT = 30000
DMA_K = 8


class Sched:
    def __init__(self):
        self.streams = {e: [] for e in ENGS}

    def add(self, eng, fn, reads=(), writes=(), dma=False):
        op = Op(eng, fn, dma)
        deps = {}
        for b in reads:
            if b.w is not None:
                deps[id(b.w)] = b.w
            if b.excl:
                for r in b.r:
                    if r.eng != eng:
                        deps[id(r)] = r
        for b in writes:
            if b.w is not None:
                deps[id(b.w)] = b.w
            for r in b.r:
                deps[id(r)] = r
        for d in deps.values():
            if d is op:
                continue
            if eng == "pe" and not dma and d.eng == "pe" and not d.dma:
                continue
            d.signal = True
            op.deps.append(d)
        for b in reads:
            if not dma:
                b.r = [r for r in b.r if r.dma or r.eng != eng]
            b.r.append(op)
        for b in writes:
            b.w = op
            b.r = []
        self.streams[eng].append(op)
        return op

    def number(self, new_sem):
        for eng in ENGS:
            sem = None
            cnt = 0
            pool = None
            uses = 0
            n = 0
            last_by_slot = [None] * DMA_K
            for i, op in enumerate(self.streams[eng]):
                op.idx = i
                if op.dma:
                    if pool is None or (n // DMA_K + 1) * 16 > SEM_LIMIT:
                        pool = [new_sem() for _ in range(DMA_K)]
                        n = 0
                        last_by_slot = [None] * DMA_K
                    slot = n % DMA_K
                    op.sem = pool[slot]
                    op.val = 16 * (n // DMA_K + 1)
                    op.prev = last_by_slot[slot]
                    last_by_slot[slot] = op
                    n += 1
                elif op.signal:
                    if sem is None or cnt >= SEM_LIMIT:
                        sem = new_sem()
                        cnt = 0
                    cnt += 1
                    op.sem = sem
                    op.val = cnt

    def emit(self, eng, e):
        seen = {}
        seen_dma = set()
        for op in self.streams[eng]:
            if op.dma and op.prev is not None and id(op.prev) not in seen_dma:
                e.wait_ge(op.prev.sem, op.prev.val)
                seen_dma.add(id(op.prev))
            for d in op.deps:
                if d.dma:
                    if id(d) in seen_dma:
                        continue
                    e.wait_ge(d.sem, d.val)
                    seen_dma.add(id(d))
                else:
                    if seen.get(d.eng, -1) >= d.idx:
                        continue
                    e.wait_ge(d.sem, d.val)
                    seen[d.eng] = d.idx
            ins = op.fn(e)
            if op.dma:
                ins.then_inc(op.sem, 16)
            elif op.signal:
                ins.then_inc(op.sem, 1)


class Arena:
    def __init__(self, words, base=0, excl=False):
        self.excl = excl
        self.words = words
        self.base = base
        self.top = base
        self.live = []
        self.retired = []

    def reset(self):
        self.retired.extend(self.live)
        self.live = []
        self.top = self.base

    def alloc(self, name, words):
        words = (words + 7) // 8 * 8
        lo = self.top
        hi = lo + words
        assert hi <= self.words, f"arena overflow allocating {name}: {hi} > {self.words}"
        self.top = hi
        b = Buf(name, lo, hi)
        b.excl = self.excl
        keep = []
        for o in self.retired:
            if o.lo < hi and lo < o.hi:
                if o.w is not None:
                    b.r.append(o.w)
                b.r.extend(o.r)
                if not (lo <= o.lo and o.hi <= hi):
                    keep.append(o)
            else:
                keep.append(o)
        self.retired = keep
        self.live.append(b)
        return b


class Builder:
    def __init__(self, T, depth=DEPTH, debug=(), stop_after=None, sbuf_words=52224, split=True):
        self.T = T
        self.NB = T // TB
        self.depth = depth
        self.split = bool(split) and depth >= 2 and (T // TB) % 2 == 0
        self.debug = set(debug)
        self.stop_after = stop_after
        self.S = Sched()
        self.sbuf_words = sbuf_words
        self.nc = bass.Bass("TRN2", target_bir_lowering=False)
        self.dram_bufs = {}

    def is_split(self, l):
        return self.split and l == self.depth - 1

    def dram_in(self, name, shape, dt):
        return self.nc.dram_tensor(name, list(shape), dt, kind="ExternalInput").ap()

    def dram_scratch(self, name, shape, dt):
        kind = "ExternalOutput" if name in self.debug else "Internal"
        return self.nc.dram_tensor(name, list(shape), dt, kind=kind).ap()

    def db(self, name, blk=0):
        key = (name, blk)
        b = self.dram_bufs.get(key)
        if b is None:
            b = Buf(f"{name}:{blk}")
            self.dram_bufs[key] = b
        return b

    def v(self, buf, dt=F32, pat=None, **kw):
        ap = self.arena_ap[:, buf.lo:buf.hi]
        if dt != F32:
            ap = ap.bitcast(dt)
        if pat is not None:
            ap = ap.rearrange(pat, **kw)
        return ap

    def pv(self, buf, dt=F32, pat=None, **kw):
        ap = self.psum_ap[:, buf.lo:buf.hi]
        if dt != F32:
            ap = ap.bitcast(dt)
        if pat is not None:
            ap = ap.rearrange(pat, **kw)
        return ap

    def load(self, out_ap, in_ap, reads, writes, eng="sp"):
        return self.S.add(eng, lambda e: e.dma_start(out=out_ap, in_=in_ap), reads, writes, dma=True)

    def store(self, out_ap, in_ap, reads, writes, eng="pool"):
        return self.S.add(eng, lambda e: e.dma_start(out=out_ap, in_=in_ap), reads, writes, dma=True)

    def mm(self, out_ap, lhsT, rhs, start, stop, reads, writes, **kw):
        return self.S.add("pe", lambda e: e.matmul(out_ap, lhsT, rhs, start=start, stop=stop, **kw), reads, writes)

    def act(self, out_ap, in_ap, func, reads, writes, bias=None, scale=None, accum_out=None):
        kw = {}
        if bias is not None:
            kw["bias"] = bias
        if scale is not None:
            kw["scale"] = scale
        if accum_out is not None:
            kw["accum_out"] = accum_out
        return self.S.add("act", lambda e: e.activation(out_ap, in_ap, func, **kw), reads, writes)

    def tt(self, eng, out_ap, a, b, op, reads, writes):
        return self.S.add(eng, lambda e: e.tensor_tensor(out_ap, a, b, op), reads, writes)

    def ts(self, eng, out_ap, a, s1, s2, op0, op1, reads, writes):
        if op1 is None:
            return self.S.add(eng, lambda e: e.tensor_scalar(out_ap, a, s1, None, op0), reads, writes)
        return self.S.add(eng, lambda e: e.tensor_scalar(out_ap, a, s1, s2, op0, op1), reads, writes)

    def stt(self, out_ap, a, s, b, op0, op1, reads, writes):
        return self.S.add("dve", lambda e: e.scalar_tensor_tensor(out_ap, a, s, b, op0, op1), reads, writes)

    def copy(self, eng, out_ap, in_ap, reads, writes):
        if eng == "act":
            return self.S.add("act", lambda e: e.copy(out_ap, in_ap), reads, writes)
        return self.S.add(eng, lambda e: e.tensor_copy(out_ap, in_ap), reads, writes)

    def memset(self, eng, ap, val, writes):
        return self.S.add(eng, lambda e: e.memset(ap, val), (), writes)

    def load_weight(self, wbuf, w_dram, K, N, stage_bufs, col_split, cast_engs=("pool", "dve", "act")):
        KC = K // 128
        cw = N // col_split
        wv = self.v(wbuf, BF16, "p (k n) -> p k n", k=KC)
        i = 0
        for k in range(KC):
            for c in range(col_split):
                st = stage_bufs[i % len(stage_bufs)]
                stv = self.v(st)[:, 0:cw]
                self.load(stv, w_dram[k * 128:(k + 1) * 128, c * cw:(c + 1) * cw], (), (st,))
                ce = cast_engs[i % len(cast_engs)]
                self.copy(ce, wv[:, k, c * cw:(c + 1) * cw], stv, (st,), (wbuf,))
                i += 1

    def build(self):
        nc = self.nc
        T = self.T
        L = self.depth
        NB = self.NB
        self.xT = self.dram_in("xT", [D, T], F32)
        self.cT = self.dram_in("cT", [128, 8], F32)
        self.pos = self.dram_in("pos", [1, T], I32)
        self.pos2 = self.dram_in("pos2", [1, T], I32)
        self.cbf = self.dram_in("cbf", [128, 384], BF16)
        self.cf32 = self.dram_in("cf32", [128, 4], F32)
        self.ada_w = self.dram_in("ada_w", [L, D, 6 * D], F32)
        self.ada_b = self.dram_in("ada_b", [128, L, 48], F32)
        self.n1g = self.dram_in("n1g", [128, L, 8], F32)
        self.n2g = self.dram_in("n2g", [128, L, 8], F32)
        self.fng = self.dram_in("fng", [128, 8], F32)
        self.w_in = self.dram_in("w_in", [L, D, NIN], F32)
        self.lamv = self.dram_in("lamv", [1, L * 4 * 64], F32)
        self.sublng = self.dram_in("sublng", [1, L * 128], F32)
        self.w_ao = self.dram_in("w_ao", [L, 512, D], F32)
        self.cw = self.dram_in("cw", [128, L, 4, CK], F32)
        self.cvec = self.dram_in("cvec", [128, L, 3, 4], F32)
        self.w_co = self.dram_in("w_co", [L, 512, D], F32)
        self.w_out = self.dram_in("w_out", [L, D, D], F32)
        self.w_f1 = self.dram_in("w_f1", [L, D, 2 * DFF], F32)
        self.w_f2 = self.dram_in("w_f2", [L, DFF, D], F32)
        self.TY = T // 2 if self.split else T
        self.yT = nc.dram_tensor("yT", [D, self.TY], F32, kind="ExternalOutput").ap()
        self.XL = self.dram_scratch("XL", [D, T], F32)
        self.CSd2 = self.dram_scratch("CSd2", [2, 128, T], F32)

        self.XS = self.dram_scratch("XS", [D, T], F32)
        self.QKB = self.dram_scratch("QKB", [D, T], BF16)
        self.Vd = self.dram_scratch("Vd", [T, 512], BF16)
        self.Ud = self.dram_scratch("Ud", [512, T], BF16)
        self.Gd = self.dram_scratch("Gd", [2 * D, T], F32)
        self.UCd = self.dram_scratch("UCd", [512, T], BF16)
        self.OTd = self.dram_scratch("OTd", [512, T], BF16)
        self.CSd = self.dram_scratch("CSd", [2, 128, T], F32)
        self.MODd = self.dram_scratch("MODd", [128, L * 48], F32)

        with ExitStack() as es:
            arena_t = es.enter_context(nc.sbuf_tensor("arena", [128, self.sbuf_words], F32))
            psum_t = es.enter_context(nc.psum_tensor("psum", [128, 4096], F32))
            self.arena_ap = arena_t[:, :] if not hasattr(arena_t, "ap") else arena_t.ap()
            self.psum_ap = psum_t[:, :] if not hasattr(psum_t, "ap") else psum_t.ap()
            self.PA = Arena(self.sbuf_words)
            self.PS = Arena(4096, excl=True)

            self.phase_setup()
            stages = ["p1", "conv", "attn", "p3a", "p3b"]
            done = False
            for l in range(L):
                for st in (["blend"] if self.is_split(l) else []) + stages:
                    getattr(self, "phase_" + st)(l)
                    if self.stop_after == (l, st):
                        done = True
                        break
                if done:
                    break

            sems = []

            def new_sem():
                s = es.enter_context(nc.semaphore(f"s{len(sems)}"))
                sems.append(s)
                return s

            self.finalize()
            self.S.number(new_sem)
            self.n_sems = len(sems)
            with nc.Block() as block:
                @block.tensor
                def _(e):
                    self.S.emit("pe", e)

                @block.scalar
                def _(e):
                    self.S.emit("act", e)

                @block.vector
                def _(e):
                    self.S.emit("dve", e)

                @block.gpsimd
                def _(e):
                    self.S.emit("pool", e)

                @block.sync
                def _(e):
                    self.S.emit("sp", e)
        return nc

    def finalize(self):
        outs = [b for (name, blk), b in self.dram_bufs.items() if name == "yT" or name in self.debug]
        fin = self.PA.alloc("fin", 8)
        self.S.add("pool", lambda e: e.memset(self.v(fin)[:, 0:1], 0.0), outs, (fin,))

    def new_phase(self):
        self.PA.reset()
        self.PS.reset()

    def phase_setup(self):
        L = self.depth
        T = self.T
        A = self.PA
        self.b_cbf = A.alloc("cbf", 192)
        self.b_ones = A.alloc("ones", 64)
        self.b_cf = A.alloc("cf32", 8)
        self.b_mod = A.alloc("mod", L * 48)
        self.b_der = A.alloc("der", L * 48)
        self.b_n1g = A.alloc("n1g", L * 8)
        self.b_n2g = A.alloc("n2g", L * 8)
        self.b_fng = A.alloc("fng", 8)
        self.b_cw = A.alloc("cw", L * 4 * CK)
        self.b_cvec = A.alloc("cvec", L * 12)
        self.b_gfac = A.alloc("gfac", L * 128)
        self.b_nlam = A.alloc("nlam", 8)
        self.b_misc = A.alloc("misc", 8)
        A.base = A.top
        v = self.v
        ld = self.load
        ld(v(self.b_cbf, BF16), self.cbf[:, :], (), (self.b_cbf,))
        ld(v(self.b_cf)[:, 0:4], self.cf32[:, :], (), (self.b_cf,))
        ld(v(self.b_n1g)[:, 0:L * 8], self.n1g.rearrange("p l k -> p (l k)"), (), (self.b_n1g,))
        ld(v(self.b_n2g)[:, 0:L * 8], self.n2g.rearrange("p l k -> p (l k)"), (), (self.b_n2g,))
        ld(v(self.b_fng)[:, 0:8], self.fng[:, :], (), (self.b_fng,))
        ld(v(self.b_cw)[:, 0:L * 4 * CK], self.cw.rearrange("p l c j -> p (l c j)"), (), (self.b_cw,))
        ld(v(self.b_cvec)[:, 0:L * 12], self.cvec.rearrange("p l a c -> p (l a c)"), (), (self.b_cvec,))
        self.memset("dve", v(self.b_ones, BF16), 1.0, (self.b_ones,))
        self.memset("dve", v(self.b_misc)[:, 0:1], -PI, (self.b_misc,))
        self.ts("dve", v(self.b_misc)[:, 1:2], v(self.b_cf)[:, 3:4], 30000.0, -30000.0, ALU.mult, ALU.add, (self.b_cf,), (self.b_misc,))
        self.ident = v(self.b_cbf, BF16)[:, 0:128]
        self.tri = v(self.b_cbf, BF16)[:, 128:256]
        self.perm = v(self.b_cbf, BF16)[:, 256:384]
        self.ones = v(self.b_ones, BF16)

        lv = A.alloc("lamv", L * 256)
        ld(v(lv)[:, 0:L * 256], self.lamv.partition_broadcast(128), (), (lv,))
        ld(v(self.b_gfac)[:, 0:L * 128], self.sublng.partition_broadcast(128), (), (self.b_gfac,))
        lt = A.alloc("lt", 64)
        ls = A.alloc("ls", 8)
        lvv = v(lv, F32, "p (l a d) -> p l a d", l=L, a=4)
        for l in range(L):
            lam_init = 0.8 - 0.6 * math.exp(-0.3 * l)
            for j in range(2):
                self.tt("dve", v(lt)[:, 0:64], lvv[:, l, 2 * j, :], lvv[:, l, 2 * j + 1, :], ALU.mult, (lv,), (lt,))
                self.S.add("dve", lambda e, j=j: e.tensor_reduce(v(ls)[:, j:j + 1], v(lt)[:, 0:64], AX.X, ALU.add),
                           (lt,), (ls,))
            self.act(v(ls)[:, 2:4], v(ls)[:, 0:2], AF.Exp, (ls,), (ls,))
            self.tt("dve", v(ls)[:, 4:5], v(ls)[:, 3:4], v(ls)[:, 2:3], ALU.subtract, (ls,), (ls,))
            self.ts("dve", v(self.b_nlam)[:, l:l + 1], v(ls)[:, 4:5], -lam_init, None, ALU.add, None, (ls,), (self.b_nlam,))
            g = v(self.b_gfac)[:, l * 128:(l + 1) * 128]
            self.ts("dve", g, g, 1.0 - lam_init, None, ALU.mult, None, (self.b_gfac,), (self.b_gfac,))

        cact = A.alloc("cact", 8)
        ld(v(cact)[:, 0:8], self.cT[:, :], (), (cact,))
        self.act(v(cact)[:, 0:8], v(cact)[:, 0:8], AF.Silu, (cact,), (cact,))
        ld(v(self.b_mod)[:, 0:L * 48], self.ada_b.rearrange("p l j -> p (l j)"), (), (self.b_mod,))
        NG = 6
        stg = [A.alloc(f"adast{i}", 8 * 1024) for i in range(2)]
        pm = self.PS.alloc("pmod", 512)
        i = 0
        for l in range(L):
            for g in range(NG):
                st = stg[i % 2]
                i += 1
                stv = v(st, F32, "p (k n) -> p k n", k=8)
                for k in range(8):
                    ld(stv[:, k, :], self.ada_w[l, k * 128:(k + 1) * 128, g * 1024:(g + 1) * 1024], (), (st,))
                for jj in range(8):
                    j = g * 8 + jj
                    col = l * 48 + j
                    for k in range(8):
                        self.mm(self.pv(pm)[:, col:col + 1], stv[:, k, jj * 128:(jj + 1) * 128], v(cact)[:, k:k + 1],
                                k == 0, k == 7, (st, cact), (pm,))
        self.tt("dve", v(self.b_mod)[:, 0:L * 48], v(self.b_mod)[:, 0:L * 48], self.pv(pm)[:, 0:L * 48], ALU.add,
                (self.b_mod, pm), (self.b_mod,))
        modv = v(self.b_mod, F32, "p (l s k) -> p l s k", l=L, s=6)
        derv = v(self.b_der, F32, "p (l s k) -> p l s k", l=L, s=6)
        n1 = v(self.b_n1g, F32, "p (l k) -> p l k", l=L)
        n2 = v(self.b_n2g, F32, "p (l k) -> p l k", l=L)
        for l in range(L):
            self.stt(derv[:, l, 0, :], modv[:, l, 1, :], 1.0, n1[:, l, :], ALU.add, ALU.mult, (self.b_mod, self.b_n1g), (self.b_der,))
            self.copy("dve", derv[:, l, 1, :], modv[:, l, 0, :], (self.b_mod,), (self.b_der,))
            self.ts("dve", derv[:, l, 2, :], modv[:, l, 2, :], 1.0, None, ALU.add, None, (self.b_mod,), (self.b_der,))
            self.stt(derv[:, l, 3, :], modv[:, l, 4, :], 1.0, n2[:, l, :], ALU.add, ALU.mult, (self.b_mod, self.b_n2g), (self.b_der,))
            self.copy("dve", derv[:, l, 4, :], modv[:, l, 3, :], (self.b_mod,), (self.b_der,))
            self.ts("dve", derv[:, l, 5, :], modv[:, l, 5, :], 1.0, None, ALU.add, None, (self.b_mod,), (self.b_der,))
        self.derv = derv
        if "MODd" in self.debug:
            self.store(self.MODd[:, :], v(self.b_mod)[:, 0:L * 48], (self.b_mod,), (self.db("MODd"),))

        CH = 2048 if T >= 2048 else T
        pi_b = A.alloc("posi", CH)
        ang = A.alloc("ang", CH)
        t1 = A.alloc("t1", CH)
        t2 = A.alloc("t2", CH)
        res = A.alloc("res", CH)
        invf = v(self.b_cf)[:, 0:1]
        nss = v(self.b_cf)[:, 1:2]
        negpi = v(self.b_misc)[:, 0:1]
        tabs = [(self.pos, self.CSd, "CSd")] + ([(self.pos2, self.CSd2, "CSd2")] if self.split else [])
        for (posap, csd, csname) in tabs:
          for c0 in range(0, T, CH):
              ld(v(pi_b).bitcast(I32)[:, 0:CH], posap[:, c0:c0 + CH].partition_broadcast(128), (), (pi_b,))
              self.copy("dve", v(ang)[:, 0:CH], v(pi_b).bitcast(I32)[:, 0:CH], (pi_b,), (ang,))
              self.ts("dve", v(ang)[:, 0:CH], v(ang)[:, 0:CH], invf, None, ALU.mult, None, (ang, self.b_cf), (ang,))
              for which in range(2):
                  a = v(t1)[:, 0:CH]
                  b = v(t2)[:, 0:CH]
                  off = PI / 2 if which == 0 else 0.0
                  self.ts("dve", a, v(ang)[:, 0:CH], off, 1.0 / (2 * PI), ALU.add, ALU.mult, (ang,), (t1,))
                  self.copy("dve", b.bitcast(I32), a, (t1,), (t2,))
                  self.copy("dve", a, b.bitcast(I32), (t2,), (t1,))
                  self.ts("dve", b, v(ang)[:, 0:CH], off, None, ALU.add, None, (ang,), (t2,))
                  self.stt(b, a, -2 * PI, b, ALU.mult, ALU.add, (t1, t2), (t2,))
                  self.ts("dve", a, b, -PI, 2 * PI, ALU.is_lt, ALU.mult, (t2,), (t1,))
                  self.tt("dve", b, b, a, ALU.add, (t1, t2), (t2,))
                  self.ts("dve", a, b, PI, -2 * PI, ALU.is_gt, ALU.mult, (t2,), (t1,))
                  self.tt("dve", b, b, a, ALU.add, (t1, t2), (t2,))
                  self.ts("dve", b, b, PI, -PI, ALU.min, ALU.max, (t2,), (t2,))
                  self.act(v(res)[:, 0:CH], b, AF.Sin, (t2,), (res,))
                  if which == 1:
                      self.ts("dve", v(res)[:, 0:CH], v(res)[:, 0:CH], nss, None, ALU.mult, None, (res, self.b_cf), (res,))
                  self.store(csd[which, :, c0:c0 + CH], v(res)[:, 0:CH], (res,), (self.db(csname, (which, c0)),))
        self.cs_ch = CH

    def phase_blend(self, l):
        self.new_phase()
        A, v = self.PA, self.v
        NBo = self.NB // 2
        b0 = v(self.b_cf)[:, 2:3]
        b1 = v(self.b_cf)[:, 3:4]
        xa = [A.alloc(f"xa{i}", 8 * TB) for i in range(2)]
        xb = [A.alloc(f"xb{i}", 8 * TB) for i in range(2)]
        ta = [A.alloc(f"bta{i}", 8 * TB) for i in range(2)]
        tb = [A.alloc(f"btb{i}", 8 * TB) for i in range(2)]
        xsd = self.XS.rearrange("(k p) t -> p k t", p=128)
        xld = self.XL.rearrange("(k p) t -> p k t", p=128)
        pat = "p (k t) -> p k t"
        for s_ in range(NBo):
            a_ = xa[s_ % 2]; b_ = xb[s_ % 2]; t_a = ta[s_ % 2]; t_b = tb[s_ % 2]
            ga, gb = 2 * s_, 2 * s_ + 1
            self.load(v(a_, F32, pat, k=8), xsd[:, :, ga * TB:(ga + 1) * TB], (self.db("XS", 2 * ga), self.db("XS", 2 * ga + 1)), (a_,))
            self.load(v(b_, F32, pat, k=8), xsd[:, :, gb * TB:(gb + 1) * TB], (self.db("XS", 2 * gb), self.db("XS", 2 * gb + 1)), (b_,))
            self.act(v(t_a), v(b_), AF.Identity, (b_, self.b_cf), (t_a,), scale=b1)
            self.act(v(t_b), v(b_), AF.Identity, (b_, self.b_cf), (t_b,), scale=b0)
            self.stt(v(t_a), v(a_), b0, v(t_a), ALU.mult, ALU.add, (a_, t_a, self.b_cf), (t_a,))
            self.stt(v(t_b), v(a_), b1, v(t_b), ALU.mult, ALU.add, (a_, t_b, self.b_cf), (t_b,))
            lo, lp = s_, NBo + s_
            self.store(xld[:, :, lo * TB:(lo + 1) * TB], v(t_a, F32, pat, k=8), (t_a,), (self.db("XL", 2 * lo), self.db("XL", 2 * lo + 1)))
            self.store(xld[:, :, lp * TB:(lp + 1) * TB], v(t_b, F32, pat, k=8), (t_b,), (self.db("XL", 2 * lp), self.db("XL", 2 * lp + 1)))

    def rmsnorm_block(self, xb, hb, sqb, rsb, ps_stat, Acol, Bcol, tmps, W=TB):
        v = self.v
        xv = v(xb, F32, "p (k t) -> p k t", k=8)
        hv = v(hb, BF16, "p (k t) -> p k t", k=8)
        sqv = v(sqb, BF16, "p (k t) -> p k t", k=8)
        for k in range(8):
            if k % 2 == 0:
                self.act(sqv[:, k, :], xv[:, k, :], AF.Square, (xb,), (sqb,))
            else:
                self.tt("pool", sqv[:, k, :], xv[:, k, :], xv[:, k, :], ALU.mult, (xb,), (sqb,))
        for k in range(8):
            self.mm(self.pv(ps_stat)[:, 0:W], self.ones, sqv[:, k, :], k == 0, k == 7, (sqb, self.b_ones), (ps_stat,))
        rs = v(rsb)[:, 0:W]
        self.ts("dve", rs, self.pv(ps_stat)[:, 0:W], 1.0 / D, 1e-6, ALU.mult, ALU.add, (ps_stat,), (rsb,))
        self.act(rs, rs, AF.Sqrt, (rsb,), (rsb,))
        self.S.add("dve", lambda e: e.reciprocal(rs, rs), (rsb,), (rsb,))
        for k in range(8):
            tb = tmps[k % len(tmps)]
            tv = v(tb)[:, 0:W]
            self.stt(tv, xv[:, k, :], Acol(k), rs, ALU.mult, ALU.mult, (xb, rsb, self.b_der), (tb,))
            self.act(hv[:, k, :], tv, AF.Identity, (tb, self.b_der), (hb,), bias=Bcol(k))

    def phase_p1(self, l):
        self.new_phase()
        A, PS, v, pv = self.PA, self.PS, self.v, self.pv
        T, NB = self.T, self.NB
        sp = self.is_split(l)
        xin = self.xT if l == 0 else (self.XL if sp else self.XS)
        xname = "xT" if l == 0 else ("XL" if sp else "XS")
        csd, csname = (self.CSd2, "CSd2") if sp else (self.CSd, "CSd")
        NBo = NB // 2 if sp else NB
        W = A.alloc("w_in", 8 * NIN // 2)
        stg = [A.alloc(f"wst{i}", 1536) for i in range(2)]
        self.load_weight(W, self.w_in[l], D, NIN, stg, 3)
        Wv = v(W, BF16, "p (k n) -> p k n", k=8)
        xb = [A.alloc(f"x{i}", 8 * TB) for i in range(2)]
        hbs = [A.alloc(f"h{i}", 8 * TB // 2) for i in range(2)]
        sqb = A.alloc("sq", 8 * TB // 2)
        tmps = [A.alloc(f"tmp{i}", TB) for i in range(2)]
        rsb = A.alloc("rs", TB)
        ostg = [A.alloc(f"o{i}", TB) for i in range(6)]
        sgb = [A.alloc(f"sg{i}", TB) for i in range(2)]
        Cb = [A.alloc(f"C{i}", TB) for i in range(2)]
        Sb = [A.alloc(f"S{i}", TB) for i in range(2)]
        zb = [A.alloc(f"zb{i}", TB // 2) for i in range(2)]
        r1 = [A.alloc(f"r1{i}", TB) for i in range(2)]
        r2 = [A.alloc(f"r2{i}", TB) for i in range(2)]
        pstat = PS.alloc("pstat", 512)
        pb = [PS.alloc(f"pb{i}", 512) for i in range(7)]
        derv = self.derv
        st = {"oi": 0, "pi": 0, "n": 0}
        xv_d = xin.rearrange("(k p) t -> p k t", p=128)
        CH = self.cs_ch

        def load_x(i):
            x = xb[i % 2]
            self.load(v(x, F32, "p (k t) -> p k t", k=8), xv_d[:, :, i * TB:(i + 1) * TB],
                      (self.db(xname, 2 * i), self.db(xname, 2 * i + 1)), (x,))

        def norm(i):
            self.rmsnorm_block(xb[i % 2], hbs[i % 2], sqb, rsb, pstat, lambda k: derv[:, l, 0, k:k + 1],
                               lambda k: derv[:, l, 1, k:k + 1], tmps)

        def nextp():
            p = pb[st["pi"] % 7]; st["pi"] += 1
            return p

        def nexto():
            o = ostg[st["oi"] % 6]; st["oi"] += 1
            return o

        load_x(0)
        if NB > 1:
            load_x(1)
        norm(0)
        for i in range(NB):
            t0 = i * TB
            hb = hbs[i % 2]
            hv = v(hb, BF16, "p (k t) -> p k t", k=8)
            C = Cb[i % 2]; Sx = Sb[i % 2]
            c0 = (t0 // CH) * CH
            self.load(v(C)[:, 0:TB], csd[0, :, t0:t0 + TB], (self.db(csname, (0, c0)),), (C,))
            self.load(v(Sx)[:, 0:TB], csd[1, :, t0:t0 + TB], (self.db(csname, (1, c0)),), (Sx,))
            full = i < NBo

            def proj(m, pbuf):
                for k in range(8):
                    self.mm(pv(pbuf), Wv[:, k, m * 128:(m + 1) * 128], hv[:, k, :], k == 0, k == 7, (W, hb), (pbuf,))

            for m in (range(8) if full else range(4, 8)):
                p = nextp()
                proj(m, p)
                n = st["n"]; st["n"] += 1
                z = zb[n % 2]; a1 = r1[n % 2]; a2 = r2[n % 2]
                zv = v(z, BF16)[:, 0:TB]
                self.copy("dve", zv, pv(p), (p,), (z,))
                p2 = nextp()
                self.mm(pv(p2), self.perm, zv, True, True, (z, self.b_cbf), (p2,))
                self.tt("dve", v(a1)[:, 0:TB], pv(p), v(C)[:, 0:TB], ALU.mult, (p, C, z), (a1,))
                self.tt("dve", v(a2)[:, 0:TB], pv(p2), v(Sx)[:, 0:TB], ALU.mult, (p2, Sx), (a2,))
                o = nexto()
                ov = v(o, BF16)[:, 0:TB]
                self.tt("pool", ov, v(a1)[:, 0:TB], v(a2)[:, 0:TB], ALU.add, (a1, a2), (o,))
                self.store(self.QKB[m * 128:(m + 1) * 128, t0:t0 + TB], ov, (o,), (self.db("QKB", (m, i)),))
            if i + 1 < NB:
                norm(i + 1)
            if i + 2 < NB:
                load_x(i + 2)
            for s_ in range(4):
                p = nextp()
                for k in range(8):
                    self.mm(pv(p), hv[:, k, s_ * 128:(s_ + 1) * 128], Wv[:, k, 1024:1536], k == 0, k == 7, (W, hb), (p,))
                o = nexto()
                ov = v(o, BF16)[:, 0:512]
                self.copy("dve" if s_ % 2 == 0 else "act", ov, pv(p), (p,), (o,))
                self.store(self.Vd[t0 + s_ * 128:t0 + (s_ + 1) * 128, :], ov, (o,), (self.db("Vd", i),))
            for c in range(4):
                p1 = nextp()
                proj(16 + c, p1)
                sg = sgb[c % 2]
                self.act(v(sg)[:, 0:TB], pv(p1), AF.Sigmoid, (p1,), (sg,))
                p2 = nextp()
                proj(12 + c, p2)
                o = nexto()
                ov = v(o, BF16)[:, 0:512]
                self.tt("dve", ov, pv(p2), v(sg)[:, 0:TB], ALU.mult, (p2, sg), (o,))
                self.store(self.Ud[c * 128:(c + 1) * 128, t0:t0 + TB], ov, (o,), (self.db("Ud", (c, i)),))
            for m in (range(16) if full else ()):
                p = nextp()
                proj(20 + m, p)
                o = nexto()
                self.act(v(o)[:, 0:TB], pv(p), AF.Sigmoid, (p,), (o,))
                self.store(self.Gd[m * 128:(m + 1) * 128, t0:t0 + TB], v(o)[:, 0:TB], (o,), (self.db("Gd", (m, i)),))

    def phase_conv(self, l):
        self.new_phase()
        A, PS, v, pv = self.PA, self.PS, self.v, self.pv
        NB = self.NB
        dg = A.alloc("diag", 4 * CK * 64)
        dgv = v(dg, BF16, "p (c j n) -> p c j n", c=4, j=CK)
        cwv = v(self.b_cw)[:, 0:self.depth * 4 * CK].rearrange("p (l c j) -> p l c j", l=self.depth, c=4)
        for c in range(4):
            for j in range(CK):
                self.ts("dve" if (c * CK + j) % 2 == 0 else "pool", dgv[:, c, j, :], self.ident, cwv[:, l, c, j:j + 1], None, ALU.mult, None,
                        (self.b_cbf, self.b_cw), (dg,))
        HW = TB + 32
        ub = [A.alloc(f"u{i}", 4 * HW // 2) for i in range(2)]
        yb = A.alloc("y", 4 * TB)
        ybf = A.alloc("ybf", 4 * TB // 2)
        ysq = A.alloc("ysq", 4 * TB // 2)
        mean = A.alloc("mean", TB)
        rstd = A.alloc("rstd", TB)
        t1 = [A.alloc(f"ct{i}", TB) for i in range(2)]
        ob = [A.alloc(f"co{i}", TB // 2) for i in range(3)]
        pc = [PS.alloc(f"pc{i}", 512) for i in range(4)]
        psum_s = PS.alloc("pss", 512)
        psum_q = PS.alloc("psq", 512)
        cv = v(self.b_cvec)[:, 0:self.depth * 12].rearrange("p (l a c) -> p l a c", l=self.depth, a=3)
        n = 0
        sp = self.is_split(l)
        NBo = NB // 2 if sp else NB
        if sp:
            hA = [A.alloc(f"hA{i}", 64) for i in range(2)]
            hB = [A.alloc(f"hB{i}", 64) for i in range(2)]
            b0 = v(self.b_cf)[:, 2:3]
            b1 = v(self.b_cf)[:, 3:4]
        for i in range(NBo):
            t0 = i * TB
            u = ub[i % 2]
            uv = v(u, BF16, "p (c t) -> p c t", c=4)
            ud = self.Ud.rearrange("(c p) t -> p c t", p=128)
            if sp:
                ha = hA[i % 2]; hb_ = hB[i % 2]
                hav = v(ha, BF16, "p (c t) -> p c t", c=4)
                hbv = v(hb_, BF16, "p (c t) -> p c t", c=4)
                la = NBo + i
                self.load(hav, ud[:, :, (la + 1) * TB - 32:(la + 1) * TB], tuple(self.db("Ud", (c, la)) for c in range(4)), (ha,))
                if i == 0:
                    self.memset("pool", hbv, 0.0, (hb_,))
                else:
                    lb_ = NBo + i - 1
                    self.load(hbv, ud[:, :, (lb_ + 1) * TB - 32:(lb_ + 1) * TB], tuple(self.db("Ud", (c, lb_)) for c in range(4)), (hb_,))
                self.ts("dve", hav, hav, b1, None, ALU.mult, None, (ha, self.b_cf), (ha,))
                self.stt(uv[:, :, 0:32], hbv, b0, hav, ALU.mult, ALU.add, (hb_, ha, self.b_cf), (u,))
                self.load(uv[:, :, 32:32 + TB], ud[:, :, t0:t0 + TB], tuple(self.db("Ud", (c, i)) for c in range(4)), (u,))
            elif i == 0:
                self.memset("pool", uv[:, :, 0:32], 0.0, (u,))
                self.load(uv[:, :, 32:32 + TB], ud[:, :, 0:TB], tuple(self.db("Ud", (c, 0)) for c in range(4)), (u,))
            else:
                self.load(uv[:, :, 0:HW], ud[:, :, t0 - 32:t0 + TB],
                          tuple(self.db("Ud", (c, ii)) for c in range(4) for ii in (i - 1, i)), (u,))
            yv = v(yb, F32, "p (c t) -> p c t", c=4)
            ybv = v(ybf, BF16, "p (c t) -> p c t", c=4)
            ysv = v(ysq, BF16, "p (c t) -> p c t", c=4)
            for c in range(4):
                p = pc[c]
                for j in range(CK):
                    self.mm(pv(p), dgv[:, c, j, :], uv[:, c, 2 + j:2 + j + TB], j == 0, j == CK - 1, (dg, u), (p,))
                self.act(yv[:, c, :], pv(p), AF.Identity, (p, self.b_cvec), (yb,), bias=cv[:, l, 0, c:c + 1])
                self.copy("pool", ybv[:, c, :], yv[:, c, :], (yb,), (ybf,))
                self.act(ysv[:, c, :], yv[:, c, :], AF.Square, (yb,), (ysq,))
            for c in range(4):
                self.mm(pv(psum_s), self.ones, ybv[:, c, :], c == 0, c == 3, (ybf, self.b_ones), (psum_s,))
            for c in range(4):
                self.mm(pv(psum_q), self.ones, ysv[:, c, :], c == 0, c == 3, (ysq, self.b_ones), (psum_q,))
            mv = v(mean)[:, 0:TB]
            rv = v(rstd)[:, 0:TB]
            self.ts("dve", mv, pv(psum_s), 1.0 / 512, None, ALU.mult, None, (psum_s,), (mean,))
            self.tt("dve", rv, mv, mv, ALU.mult, (mean,), (rstd,))
            self.stt(rv, pv(psum_q), 1.0 / 512, rv, ALU.mult, ALU.subtract, (psum_q, rstd), (rstd,))
            self.ts("dve", rv, rv, 1e-5, None, ALU.add, None, (rstd,), (rstd,))
            self.act(rv, rv, AF.Sqrt, (rstd,), (rstd,))
            self.S.add("dve", lambda e, rv=rv: e.reciprocal(rv, rv), (rstd,), (rstd,))
            for c in range(4):
                ta = t1[n % 2]; o = ob[n % 3]; n += 1
                tav = v(ta)[:, 0:TB]
                self.tt("dve", tav, yv[:, c, :], mv, ALU.subtract, (yb, mean), (ta,))
                self.tt("pool", tav, tav, rv, ALU.mult, (ta, rstd), (ta,))
                ov = v(o, BF16)[:, 0:TB]
                self.act(ov, tav, AF.Silu, (ta, self.b_cvec), (o,), scale=cv[:, l, 1, c:c + 1], bias=cv[:, l, 2, c:c + 1])
                self.store(self.UCd[c * 128:(c + 1) * 128, t0:t0 + TB], ov, (o,), (self.db("UCd", (c, i)),))

    def phase_attn(self, l):
        self.new_phase()
        A, PS, v, pv = self.PA, self.PS, self.v, self.pv
        T, NB = self.T, self.NB
        NT = T // 128
        scale = 1.0 / 8.0
        QTb = [A.alloc(f"QT{i}", T // 2) for i in range(2)]
        KTb = [A.alloc(f"KT{i}", T // 2) for i in range(2)]
        V1b = [A.alloc(f"V1{i}", NT * 130 // 2) for i in range(2)]
        ptb = [[A.alloc(f"pt{c}{i}", TB // 2) for i in range(3)] for c in range(2)]
        osb = [A.alloc(f"os{i}", 128) for i in range(2)]
        t2b = [A.alloc(f"t2{i}", 128) for i in range(2)]
        sm = [A.alloc(f"sm{i}", 8) for i in range(2)]
        oT = [A.alloc(f"oT{i}", TB // 2) for i in range(2)]
        psS = [[PS.alloc(f"pS{c}{i}", 512) for i in range(2)] for c in range(2)]
        accbank = [PS.alloc(f"acc{i}", 512) for i in range(3)]
        ptr = PS.alloc("ptr", 512)
        gfac = v(self.b_gfac)[:, l * 128:(l + 1) * 128]
        nlam = v(self.b_nlam)[:, l:l + 1]
        st = {"nS": 0, "nE": 0}

        def acc(c, s):
            idx = c * 4 + s
            bk = accbank[idx // 3]
            off = (idx % 3) * 160
            return bk, pv(bk)[:, off:off + 129]

        def load_head(h):
            QT = QTb[h % 2]; KT = KTb[h % 2]; V1 = V1b[h % 2]
            QTv = v(QT, BF16)[:, 0:T]
            KTv = v(KT, BF16)[:, 0:T]
            V1v = v(V1, BF16, "p (n d) -> p n d", d=130)
            self.load(QTv, self.QKB[h * 128:(h + 1) * 128, :], tuple(self.db("QKB", (h, i)) for i in range(NB)), (QT,))
            self.load(KTv, self.QKB[512 + h * 128:512 + (h + 1) * 128, :], tuple(self.db("QKB", (4 + h, i)) for i in range(NB)), (KT,))
            vdv = self.Vd.rearrange("(n p) d -> p n d", p=128)
            for n0 in range(0, NT, 8):
                self.load(V1v[:, n0:n0 + 8, 0:128], vdv[:, n0:n0 + 8, h * 128:(h + 1) * 128],
                          tuple(self.db("Vd", i) for i in range(n0 // 4, n0 // 4 + 2)), (V1,))
            self.memset("pool", V1v[:, :, 128:130], 1.0, (V1,))

        sp = self.is_split(l)
        NBo = NB // 2 if sp else NB
        mbias = v(self.b_misc)[:, 1:2]
        accs = [[[A.alloc(f"accs{i}{c}{s_}", 136) for s_ in range(4)] for c in range(2)] for i in range(2)]
        onb = [[A.alloc(f"onb{i}{s_}", 64) for s_ in range(4)] for i in range(2)]

        def block_iters(i):
            its = [(kt, "c", kt - 4 * i) for kt in range(4 * i + 4)]
            if sp:
                base = 4 * NBo
                its += [(base + kt, "f", -1) for kt in range(4 * i)]
                its += [(base + 4 * i + kt, "m", -1) for kt in range(4)]
            return its

        load_head(0)
        for h in range(NH):
            if h + 1 < NH:
                load_head(h + 1)
            QT = QTb[h % 2]; KT = KTb[h % 2]; V1 = V1b[h % 2]
            QTv = v(QT, BF16)[:, 0:T]
            KTv = v(KT, BF16)[:, 0:T]
            V1v = v(V1, BF16, "p (n d) -> p n d", d=130)

            def emit_qk(i, kt, mode, sd):
                q0 = i * TB
                f0 = max(sd, 0) * 128 if mode == "c" else 0
                nS = st["nS"]; st["nS"] += 1
                pts = []
                for c in range(2):
                    pS = psS[c][nS % 2]
                    r0 = c * 64
                    self.mm(pv(pS)[:, f0:TB], KTv[r0:r0 + 64, kt * 128:(kt + 1) * 128], QTv[r0:r0 + 64, q0 + f0:q0 + TB],
                            True, True, (KT, QT), (pS,))
                    pt = ptb[c][nS % 3]
                    ptv = v(pt, BF16)[:, 0:TB]
                    if mode == "m":
                        self.act(ptv[:, f0:TB], pv(pS)[:, f0:TB], AF.Exp, (pS, self.b_misc), (pt,), scale=scale, bias=mbias)
                    else:
                        self.act(ptv[:, f0:TB], pv(pS)[:, f0:TB], AF.Exp, (pS,), (pt,), scale=scale)
                    if mode == "c" and sd >= 0:
                        self.tt("pool", ptv[:, f0:f0 + 128], ptv[:, f0:f0 + 128], self.tri, ALU.mult, (pt, self.b_cbf), (pt,))
                    pts.append((pt, ptv))
                return pts, f0 // 128

            def emit_pv(kt, pts, smin):
                for s in range(smin, 4):
                    for c in range(2):
                        bk, av = acc(c, s)
                        pt, ptv = pts[c]
                        self.mm(av, ptv[:, s * 128:(s + 1) * 128], V1v[:, kt, 0:129], False, True,
                                (pt, V1), (bk,), skip_group_check=True)

            def epilogue_a(i):
                cps = []
                for s in range(4):
                    row = []
                    for c in range(2):
                        bk, av = acc(c, s)
                        ab = accs[i % 2][c][s]
                        abv = v(ab)[:, 0:129]
                        self.copy("dve", abv, av, (bk,), (ab,))
                        row.append((ab, abv))
                    cps.append(row)
                zero_acc()
                first = cps[0][0][0]
                for s in range(4):
                    (b1_, a1), (b2_, a2) = cps[s]
                    nE = st["nE"]; st["nE"] += 1
                    smb = sm[nE % 2]; os_ = osb[nE % 2]; t2 = t2b[nE % 2]; on = onb[i % 2][s]
                    smv = v(smb)
                    self.S.add("dve", lambda e, smv=smv, a1=a1: e.reciprocal(smv[:, 0:1], a1[:, 128:129]), (b1_,), (smb,))
                    self.S.add("dve", lambda e, smv=smv, a2=a2: e.reciprocal(smv[:, 1:2], a2[:, 128:129]), (b2_,), (smb,))
                    self.tt("dve", smv[:, 1:2], smv[:, 1:2], nlam, ALU.mult, (smb, self.b_nlam), (smb,))
                    t2v = v(t2)[:, 0:128]
                    osv = v(os_)[:, 0:128]
                    self.ts("dve", t2v, a2[:, 0:128], smv[:, 1:2], None, ALU.mult, None, (b2_, smb), (t2,))
                    self.stt(osv, a1[:, 0:128], smv[:, 0:1], t2v, ALU.mult, ALU.add, (b1_, smb, t2), (os_,))
                    self.act(t2v, osv, AF.Square, (os_,), (t2,))
                    self.S.add("dve", lambda e, smv=smv, t2v=t2v: e.tensor_reduce(smv[:, 2:3], t2v, AX.X, ALU.add), (t2,), (smb,))
                    self.ts("dve", smv[:, 2:3], smv[:, 2:3], 1.0 / 128, 1e-5, ALU.mult, ALU.add, (smb,), (smb,))
                    self.act(smv[:, 2:3], smv[:, 2:3], AF.Sqrt, (smb,), (smb,))
                    self.S.add("dve", lambda e, smv=smv: e.reciprocal(smv[:, 2:3], smv[:, 2:3]), (smb,), (smb,))
                    onv = v(on, BF16)[:, 0:128]
                    self.stt(onv, osv, smv[:, 2:3], gfac, ALU.mult, ALU.mult, (os_, smb, self.b_gfac), (on,))
                return first

            def epilogue_b(i, drain=None):
                q0 = i * TB
                oTb = oT[i % 2]
                oTv = v(oTb, BF16)[:, 0:TB]
                for s in range(4):
                    on = onb[i % 2][s]
                    onv = v(on, BF16)[:, 0:128]
                    ptv_ = pv(ptr).bitcast(BF16)[:, s * 128:(s + 1) * 128]
                    rd = (on, self.b_cbf) + ((drain,) if (drain is not None and s == 0) else ())
                    self.S.add("pe", lambda e, ptv_=ptv_, onv=onv: e.transpose(ptv_, onv, self.ident), rd, (ptr,))
                self.copy("act", oTv, pv(ptr).bitcast(BF16)[:, 0:TB], (ptr,), (oTb,))
                self.store(self.OTd[h * 128:(h + 1) * 128, q0:q0 + TB], oTv, (oTb,), (self.db("OTd", (h, i)),))

            def zero_acc():
                for bk in accbank:
                    self.memset("dve", pv(bk), 0.0, (bk,))

            iters = []
            for i in range(NBo):
                bi = block_iters(i)
                for n_, (kt, mode, sd) in enumerate(bi):
                    iters.append((i, kt, mode, sd, n_ == len(bi) - 1))
            prev = None
            pending = []
            zero_acc()
            for (i, kt, mode, sd, lastk) in iters:
                pts, smin = emit_qk(i, kt, mode, sd)
                if prev is not None:
                    pi_, pk_, pp_, ps_, pl_ = prev
                    emit_pv(pk_, pp_, ps_)
                    if pl_:
                        dr = epilogue_a(pi_)
                        if DEFER_TR:
                            if pending:
                                epilogue_b(pending.pop(0), dr)
                            pending.append(pi_)
                        else:
                            epilogue_b(pi_)
                prev = (i, kt, pts, smin, lastk)
            pi_, pk_, pp_, ps_, pl_ = prev
            emit_pv(pk_, pp_, ps_)
            dr = epilogue_a(pi_)
            while pending:
                epilogue_b(pending.pop(0), dr)
            epilogue_b(pi_)

    def phase_p3a(self, l):
        self.new_phase()
        A, PS, v, pv = self.PA, self.PS, self.v, self.pv
        NB = self.NB
        derv = self.derv
        sp = self.is_split(l)
        xin = self.xT if l == 0 else (self.XL if sp else self.XS)
        xname = "xT" if l == 0 else ("XL" if sp else "XS")
        xout, xoname = (self.XL, "XL") if sp else (self.XS, "XS")
        if sp:
            NB = NB // 2
        Wa = A.alloc("w_ao", 4 * D // 2)
        Wc = A.alloc("w_co", 4 * D // 2)
        Wo = A.alloc("w_o", 8 * D // 2)
        stg = [A.alloc(f"wst{i}", 1024) for i in range(3)]
        self.load_weight(Wa, self.w_ao[l], 512, D, stg, 1)
        self.load_weight(Wc, self.w_co[l], 512, D, stg, 1)
        self.load_weight(Wo, self.w_out[l], D, D, stg, 1)
        Wav = v(Wa, BF16, "p (k n) -> p k n", k=4)
        Wcv = v(Wc, BF16, "p (k n) -> p k n", k=4)
        Wov = v(Wo, BF16, "p (k n) -> p k n", k=8)
        xb = [A.alloc(f"x{i}", 8 * TB) for i in range(2)]
        gb = [A.alloc(f"g{i}", 16 * TB) for i in range(2)]
        otb = [A.alloc(f"ot{i}", 4 * TB // 2) for i in range(2)]
        ucb = [A.alloc(f"uc{i}", 4 * TB // 2) for i in range(2)]
        mixb = A.alloc("mix", 8 * TB // 2)
        ta = [A.alloc(f"ta{i}", TB) for i in range(2)]
        tb_ = [A.alloc(f"tb{i}", TB) for i in range(2)]
        pa = [PS.alloc(f"pa{i}", 512) for i in range(2)]
        pc = [PS.alloc(f"pc{i}", 512) for i in range(2)]
        po = [PS.alloc(f"po{i}", 512) for i in range(2)]
        n = 0
        for i in range(NB):
            t0 = i * TB
            x = xb[i % 2]; g = gb[i % 2]; ot = otb[i % 2]; uc = ucb[i % 2]
            xv = v(x, F32, "p (k t) -> p k t", k=8)
            gv = v(g, F32, "p (k t) -> p k t", k=16)
            otv = v(ot, BF16, "p (k t) -> p k t", k=4)
            ucv = v(uc, BF16, "p (k t) -> p k t", k=4)
            self.load(otv, self.OTd.rearrange("(k p) t -> p k t", p=128)[:, :, t0:t0 + TB],
                      tuple(self.db("OTd", (h, i)) for h in range(4)), (ot,))
            self.load(ucv, self.UCd.rearrange("(k p) t -> p k t", p=128)[:, :, t0:t0 + TB],
                      tuple(self.db("UCd", (c, i)) for c in range(4)), (uc,))
            self.load(gv[:, 0:8, :], self.Gd.rearrange("(k p) t -> p k t", p=128)[:, 0:8, t0:t0 + TB],
                      tuple(self.db("Gd", (m, i)) for m in range(8)), (g,))
            self.load(gv[:, 8:16, :], self.Gd.rearrange("(k p) t -> p k t", p=128)[:, 8:16, t0:t0 + TB],
                      tuple(self.db("Gd", (m, i)) for m in range(8, 16)), (g,))
            self.load(xv, xin.rearrange("(k p) t -> p k t", p=128)[:, :, t0:t0 + TB], (self.db(xname, 2 * i), self.db(xname, 2 * i + 1)), (x,))
            mixv = v(mixb, BF16, "p (k t) -> p k t", k=8)
            for m in range(8):
                p1 = pa[m % 2]; p2 = pc[m % 2]
                for k in range(4):
                    self.mm(pv(p1), Wav[:, k, m * 128:(m + 1) * 128], otv[:, k, :], k == 0, k == 3, (Wa, ot), (p1,))
                for k in range(4):
                    self.mm(pv(p2), Wcv[:, k, m * 128:(m + 1) * 128], ucv[:, k, :], k == 0, k == 3, (Wc, uc), (p2,))
                a = ta[n % 2]; b = tb_[n % 2]; n += 1
                av = v(a)[:, 0:TB]; bv = v(b)[:, 0:TB]
                self.tt("dve", av, pv(p1), gv[:, m, :], ALU.mult, (p1, g), (a,))
                self.tt("dve", bv, pv(p2), gv[:, 8 + m, :], ALU.mult, (p2, g), (b,))
                self.tt("pool", mixv[:, m, :], av, bv, ALU.add, (a, b), (mixb,))
            for m in range(8):
                p = po[m % 2]
                for k in range(8):
                    self.mm(pv(p), Wov[:, k, m * 128:(m + 1) * 128], mixv[:, k, :], k == 0, k == 7, (Wo, mixb), (p,))
                self.stt(xv[:, m, :], pv(p), derv[:, l, 2, m:m + 1], xv[:, m, :], ALU.mult, ALU.add, (p, x, self.b_der), (x,))
            self.store(xout.rearrange("(k p) t -> p k t", p=128)[:, :, t0:t0 + TB], xv, (x,), (self.db(xoname, 2 * i), self.db(xoname, 2 * i + 1)))

    def phase_p3b(self, l):
        self.new_phase()
        A, PS, v, pv = self.PA, self.PS, self.v, self.pv
        FB = 256
        NBF = self.T // FB
        derv = self.derv
        last = (l == self.depth - 1)
        sp = self.is_split(l)
        xsrc, xsname = (self.XL, "XL") if sp else (self.XS, "XS")
        if sp:
            NBF = NBF // 2
        KF = DFF // 128
        W1 = A.alloc("w_f1", 8 * 2 * DFF // 2)
        W2 = A.alloc("w_f2", KF * D // 2)
        stg = [A.alloc(f"wst{i}", 704) for i in range(2)]
        self.load_weight(W1, self.w_f1[l], D, 2 * DFF, stg, 8)
        self.load_weight(W2, self.w_f2[l], DFF, D, stg, 2)
        W1v = v(W1, BF16, "p (k n) -> p k n", k=8)
        W2v = v(W2, BF16, "p (k n) -> p k n", k=KF)
        xb = [A.alloc(f"x{i}", 8 * FB) for i in range(2)]
        hbs = [A.alloc(f"h{i}", 8 * FB // 2) for i in range(2)]
        sqb = A.alloc("sq", 8 * FB // 2)
        tmps = [A.alloc(f"tmp{i}", FB) for i in range(2)]
        yb = A.alloc("yb", 8 * FB) if last else None
        rsb = A.alloc("rs", FB)
        gub = A.alloc("gu", KF * FB // 2)
        sgb = [A.alloc(f"sg{i}", FB) for i in range(2)]
        pstat = PS.alloc("pstat", 512)
        pg = [PS.alloc(f"pg{i}", 512) for i in range(2)]
        pu = [PS.alloc(f"pu{i}", 512) for i in range(2)]
        po = [PS.alloc(f"po{i}", 512) for i in range(2)]
        fng = v(self.b_fng)
        def load_x(i):
            x = xb[i % 2]
            self.load(v(x, F32, "p (k t) -> p k t", k=8), xsrc.rearrange("(k p) t -> p k t", p=128)[:, :, i * FB:(i + 1) * FB],
                      (self.db(xsname, i),), (x,))

        def norm(i):
            self.rmsnorm_block(xb[i % 2], hbs[i % 2], sqb, rsb, pstat, lambda k: derv[:, l, 3, k:k + 1],
                               lambda k: derv[:, l, 4, k:k + 1], tmps, W=FB)

        load_x(0)
        if NBF > 1:
            load_x(1)
        norm(0)
        for i in range(NBF):
            t0 = i * FB
            x = xb[i % 2]
            xv = v(x, F32, "p (k t) -> p k t", k=8)
            hb = hbs[i % 2]
            hv = v(hb, BF16, "p (k t) -> p k t", k=8)
            guv = v(gub, BF16, "p (k t) -> p k t", k=KF)
            for m in range(KF):
                if m == KF // 2 and i + 1 < NBF:
                    norm(i + 1)
                p1 = pg[m % 2]; p2 = pu[m % 2]; sg = sgb[m % 2]
                for k in range(8):
                    self.mm(pv(p1)[:, 0:FB], W1v[:, k, m * 128:(m + 1) * 128], hv[:, k, :], k == 0, k == 7, (W1, hb), (p1,))
                for k in range(8):
                    self.mm(pv(p2)[:, 0:FB], W1v[:, k, DFF + m * 128:DFF + (m + 1) * 128], hv[:, k, :], k == 0, k == 7, (W1, hb), (p2,))
                self.act(v(sg)[:, 0:FB], pv(p1)[:, 0:FB], AF.Silu, (p1,), (sg,))
                self.tt("dve", guv[:, m, :], pv(p2)[:, 0:FB], v(sg)[:, 0:FB], ALU.mult, (p2, sg), (gub,))
            for m in range(8):
                p = po[m % 2]
                for k in range(KF):
                    self.mm(pv(p)[:, 0:FB], W2v[:, k, m * 128:(m + 1) * 128], guv[:, k, :], k == 0, k == KF - 1, (W2, gub), (p,))
                self.stt(xv[:, m, :], pv(p)[:, 0:FB], derv[:, l, 5, m:m + 1], xv[:, m, :], ALU.mult, ALU.add, (p, x, self.b_der), (x,))
            if not last:
                self.store(self.XS.rearrange("(k p) t -> p k t", p=128)[:, :, t0:t0 + FB], xv, (x,), (self.db("XS", i),))
            else:
                sqv = v(sqb, BF16, "p (k t) -> p k t", k=8)
                for k in range(8):
                    if k % 2 == 0:
                        self.act(sqv[:, k, :], xv[:, k, :], AF.Square, (x,), (sqb,))
                    else:
                        self.tt("pool", sqv[:, k, :], xv[:, k, :], xv[:, k, :], ALU.mult, (x,), (sqb,))
                for k in range(8):
                    self.mm(pv(pstat)[:, 0:FB], self.ones, sqv[:, k, :], k == 0, k == 7, (sqb, self.b_ones), (pstat,))
                rs = v(rsb)[:, 0:FB]
                self.ts("dve", rs, pv(pstat)[:, 0:FB], 1.0 / D, 1e-6, ALU.mult, ALU.add, (pstat,), (rsb,))
                self.act(rs, rs, AF.Sqrt, (rsb,), (rsb,))
                self.S.add("dve", lambda e, rs=rs: e.reciprocal(rs, rs), (rsb,), (rsb,))
                yv = v(yb, F32, "p (k t) -> p k t", k=8)
                for k in range(8):
                    self.stt(yv[:, k, :], xv[:, k, :], fng[:, k:k + 1], rs, ALU.mult, ALU.mult, (x, rsb, self.b_fng), (yb,))
                self.store(self.yT.rearrange("(k p) t -> p k t", p=128)[:, :, t0:t0 + FB], yv, (yb,), (self.db("yT", i),))
            if i + 2 < NBF:
                load_x(i + 2)


def _pk(vec, k):
    return np.ascontiguousarray(np.asarray(vec, np.float32).reshape(k, 128).T)


def make_consts():
    ident = np.eye(128, dtype=np.float32)
    tri = (np.arange(128)[None, :] >= np.arange(128)[:, None]).astype(np.float32)
    perm = np.zeros((128, 128), np.float32)
    for r in range(128):
        rr = r % 64
        if rr < 8:
            perm[r + 8, r] = 1.0
        elif rr < 16:
            perm[r - 8, r] = 1.0
    cbf = np.concatenate([ident, tri, perm], 1).astype(ml_dtypes.bfloat16)
    inv_freq = 1.0 / (500000.0 ** (np.arange(0, 16, 2, dtype=np.float64) / 16))
    cf = np.zeros((128, 4), np.float32)
    for r in range(128):
        rr = r % 64
        if rr < 16:
            cf[r, 0] = np.float32(inv_freq[rr % 8])
            cf[r, 1] = -1.0 if rr < 8 else 1.0
    return cbf, cf


def prep_inputs(inp, T, L):
    f = lambda a: np.ascontiguousarray(np.asarray(a, np.float32))
    cbf, cf = make_consts()
    sh = {
        "cbf": cbf, "cf32": cf,
        "ada_w": f(inp["ada_w"])[:L],
        "ada_b": np.ascontiguousarray(np.stack([_pk(inp["ada_b"][l], 48) for l in range(L)], 1)),
        "n1g": np.ascontiguousarray(np.stack([_pk(inp["norm1_g"][l], 8) for l in range(L)], 1)),
        "n2g": np.ascontiguousarray(np.stack([_pk(inp["norm2_g"][l], 8) for l in range(L)], 1)),
        "fng": _pk(inp["final_g"], 8),
        "w_in": f(inp["w_in"])[:L],
        "lamv": np.ascontiguousarray(np.stack([np.stack([f(inp[n])[l] for n in ("lambda_q1", "lambda_k1", "lambda_q2", "lambda_k2")], 0)
                                               for l in range(L)], 0).reshape(1, -1)),
        "sublng": f(inp["subln_g"])[:L].reshape(1, -1),
        "w_ao": f(inp["w_attn_o"])[:L],
        "cw": np.ascontiguousarray(np.stack([f(inp["dw_conv_w"])[l][:, 0, :].reshape(CK, 4, 128).transpose(2, 1, 0) for l in range(L)], 1)),
        "cvec": np.ascontiguousarray(np.stack([np.stack([_pk(inp[n][l], 4) for n in ("dw_conv_b", "conv_ln_g", "conv_ln_b")], 1)
                                               for l in range(L)], 1)),
        "w_co": f(inp["w_conv_o"])[:L],
        "w_out": f(inp["w_out"])[:L],
        "w_f1": f(inp["w_ffn_in"])[:L],
        "w_f2": f(inp["w_ffn_out"])[:L],
    }
    return sh


def core_inputs(inp, sh, b, j, T, split):
    d = dict(sh)
    d["xT"] = np.ascontiguousarray(np.asarray(inp["x"][b], np.float32)[:T].T)
    d["cT"] = _pk(inp["c"][b], 8)
    pos = np.asarray(inp["positions"][b], np.int32)[:T]
    d["pos"] = np.ascontiguousarray(pos.reshape(1, T))
    if split:
        nb = T // TB
        order = [2 * s + j for s in range(nb // 2)] + [2 * s + 1 - j for s in range(nb // 2)]
        d["pos2"] = np.ascontiguousarray(pos.reshape(nb, TB)[order].reshape(1, T))
    else:
        d["pos2"] = d["pos"]
    cf = sh["cf32"].copy()
    cf[:, 2] = 1.0 - j
    cf[:, 3] = float(j)
    d["cf32"] = cf
    return d


def assemble(results, B, T, split):
    out = np.empty((B, T, D), np.float32)
    for b in range(B):
        if not split:
            out[b] = np.asarray(results[2 * b]["yT"], np.float32).T
        else:
            nb = T // TB
            for j in range(2):
                y = np.asarray(results[2 * b + j]["yT"], np.float32)
                for s in range(nb // 2):
                    g = 2 * s + j
                    out[b, g * TB:(g + 1) * TB] = y[:, s * TB:(s + 1) * TB].T
    return out


_CACHE = {}


def kernel(**inputs):
    B, T, _ = inputs["x"].shape
    L = DEPTH
    key = (T, L)
    if key not in _CACHE:
        bld = Builder(T, L)
        _CACHE[key] = (bld.build(), bld.split)
    nc, split = _CACHE[key]
    sh = prep_inputs(inputs, T, L)
    in_maps = [core_inputs(inputs, sh, c // 2, c % 2, T, split) for c in range(8)]
    res = run_bass_kernel_spmd(nc, in_maps, core_ids=list(range(8)))
    return assemble(res.results, B, T, split)
```

```python
import math
from contextlib import ExitStack

import numpy as np
import ml_dtypes
import concourse.bass as bass
import concourse.mybir as mybir
from concourse.bass_utils import run_bass_kernel_spmd

F32 = mybir.dt.float32
BF16 = mybir.dt.bfloat16
I32 = mybir.dt.int32
AF = mybir.ActivationFunctionType
ALU = mybir.AluOpType
AX = mybir.AxisListType

D = 1024
DEPTH = 2
NH = 4
NIN = 4608
DFF = 2816
CK = 31
TB = 512
DEFER_TR = 1
PI = math.pi


class Buf:
    __slots__ = ("name", "lo", "hi", "w", "r", "excl")

    def __init__(self, name, lo=0, hi=0):
        self.name = name
        self.lo = lo
        self.hi = hi
        self.w = None
        self.r = []
        self.excl = False


class Op:
    __slots__ = ("eng", "fn", "deps", "signal", "idx", "dma", "sem", "val", "prev")

    def __init__(self, eng, fn, dma):
        self.eng = eng
        self.fn = fn
        self.dma = dma
        self.deps = []
        self.signal = False
        self.idx = -1
        self.sem = None
        self.val = 0
        self.prev = None


ENGS = ("pe", "act", "dve", "pool", "sp")
SEM_LIMIT = 30000
DMA_K = 8


class Sched:
    def __init__(self):
        self.streams = {e: [] for e in ENGS}

    def add(self, eng, fn, reads=(), writes=(), dma=False):
        op = Op(eng, fn, dma)
        deps = {}
        for b in reads:
            if b.w is not None:
                deps[id(b.w)] = b.w
            if b.excl:
                for r in b.r:
                    if r.eng != eng:
                        deps[id(r)] = r
        for b in writes:
            if b.w is not None:
                deps[id(b.w)] = b.w
            for r in b.r:
                deps[id(r)] = r
        for d in deps.values():
            if d is op:
                continue
            if eng == "pe" and not dma and d.eng == "pe" and not d.dma:
                continue
            d.signal = True
            op.deps.append(d)
        for b in reads:
            if not dma:
                b.r = [r for r in b.r if r.dma or r.eng != eng]
            b.r.append(op)
        for b in writes:
            b.w = op
            b.r = []
        self.streams[eng].append(op)
        return op

    def number(self, new_sem):
        for eng in ENGS:
            sem = None
            cnt = 0
            pool = None
            uses = 0
            n = 0
            last_by_slot = [None] * DMA_K
            for i, op in enumerate(self.streams[eng]):
                op.idx = i
                if op.dma:
                    if pool is None or (n // DMA_K + 1) * 16 > SEM_LIMIT:
                        pool = [new_sem() for _ in range(DMA_K)]
                        n = 0
                        last_by_slot = [None] * DMA_K
                    slot = n % DMA_K
                    op.sem = pool[slot]
                    op.val = 16 * (n // DMA_K + 1)
                    op.prev = last_by_slot[slot]
                    last_by_slot[slot] = op
                    n += 1
                elif op.signal:
                    if sem is None or cnt >= SEM_LIMIT:
                        sem = new_sem()
                        cnt = 0
                    cnt += 1
                    op.sem = sem
                    op.val = cnt

    def emit(self, eng, e):
        seen = {}
        seen_dma = set()
        for op in self.streams[eng]:
            if op.dma and op.prev is not None and id(op.prev) not in seen_dma:
                e.wait_ge(op.prev.sem, op.prev.val)
                seen_dma.add(id(op.prev))
            for d in op.deps:
                if d.dma:
                    if id(d) in seen_dma:
                        continue
                    e.wait_ge(d.sem, d.val)
                    seen_dma.add(id(d))
                else:
                    if seen.get(d.eng, -1) >= d.idx:
                        continue
                    e.wait_ge(d.sem, d.val)
                    seen[d.eng] = d.idx
            ins = op.fn(e)
            if op.dma:
                ins.then_inc(op.sem, 16)
            elif op.signal:
                ins.then_inc(op.sem, 1)


class Arena:
    def __init__(self, words, base=0, excl=False):
        self.excl = excl
        self.words = words
        self.base = base
        self.top = base
        self.live = []
        self.retired = []

    def reset(self):
        self.retired.extend(self.live)
        self.live = []
        self.top = self.base

    def alloc(self, name, words):
        words = (words + 7) // 8 * 8
        lo = self.top
        hi = lo + words
        assert hi <= self.words, f"arena overflow allocating {name}: {hi} > {self.words}"
        self.top = hi
        b = Buf(name, lo, hi)
        b.excl = self.excl
        keep = []
        for o in self.retired:
            if o.lo < hi and lo < o.hi:
                if o.w is not None:
                    b.r.append(o.w)
                b.r.extend(o.r)
                if not (lo <= o.lo and o.hi <= hi):
                    keep.append(o)
            else:
                keep.append(o)
        self.retired = keep
        self.live.append(b)
        return b


class Builder:
    def __init__(self, T, depth=DEPTH, debug=(), stop_after=None, sbuf_words=52224, split=True):
        self.T = T
        self.NB = T // TB
        self.depth = depth
        self.split = bool(split) and depth >= 2 and (T // TB) % 2 == 0
        self.debug = set(debug)
        self.stop_after = stop_after
        self.S = Sched()
        self.sbuf_words = sbuf_words
        self.nc = bass.Bass("TRN2", target_bir_lowering=False)
        self.dram_bufs = {}

    def is_split(self, l):
        return self.split and l == self.depth - 1

    def dram_in(self, name, shape, dt):
        return self.nc.dram_tensor(name, list(shape), dt, kind="ExternalInput").ap()

    def dram_scratch(self, name, shape, dt):
        kind = "ExternalOutput" if name in self.debug else "Internal"
        return self.nc.dram_tensor(name, list(shape), dt, kind=kind).ap()

    def db(self, name, blk=0):
        key = (name, blk)
        b = self.dram_bufs.get(key)
        if b is None:
            b = Buf(f"{name}:{blk}")
            self.dram_bufs[key] = b
        return b

    def v(self, buf, dt=F32, pat=None, **kw):
        ap = self.arena_ap[:, buf.lo:buf.hi]
        if dt != F32:
            ap = ap.bitcast(dt)
        if pat is not None:
            ap = ap.rearrange(pat, **kw)
        return ap

    def pv(self, buf, dt=F32, pat=None, **kw):
        ap = self.psum_ap[:, buf.lo:buf.hi]
        if dt != F32:
            ap = ap.bitcast(dt)
        if pat is not None:
            ap = ap.rearrange(pat, **kw)
        return ap

    def load(self, out_ap, in_ap, reads, writes, eng="sp"):
        return self.S.add(eng, lambda e: e.dma_start(out=out_ap, in_=in_ap), reads, writes, dma=True)

    def store(self, out_ap, in_ap, reads, writes, eng="pool"):
        return self.S.add(eng, lambda e: e.dma_start(out=out_ap, in_=in_ap), reads, writes, dma=True)

    def mm(self, out_ap, lhsT, rhs, start, stop, reads, writes, **kw):
        return self.S.add("pe", lambda e: e.matmul(out_ap, lhsT, rhs, start=start, stop=stop, **kw), reads, writes)

    def act(self, out_ap, in_ap, func, reads, writes, bias=None, scale=None, accum_out=None):
        kw = {}
        if bias is not None:
            kw["bias"] = bias
        if scale is not None:
            kw["scale"] = scale
        if accum_out is not None:
            kw["accum_out"] = accum_out
        return self.S.add("act", lambda e: e.activation(out_ap, in_ap, func, **kw), reads, writes)

    def tt(self, eng, out_ap, a, b, op, reads, writes):
        return self.S.add(eng, lambda e: e.tensor_tensor(out_ap, a, b, op), reads, writes)

    def ts(self, eng, out_ap, a, s1, s2, op0, op1, reads, writes):
        if op1 is None:
            return self.S.add(eng, lambda e: e.tensor_scalar(out_ap, a, s1, None, op0), reads, writes)
        return self.S.add(eng, lambda e: e.tensor_scalar(out_ap, a, s1, s2, op0, op1), reads, writes)

    def stt(self, out_ap, a, s, b, op0, op1, reads, writes):
        return self.S.add("dve", lambda e: e.scalar_tensor_tensor(out_ap, a, s, b, op0, op1), reads, writes)

    def copy(self, eng, out_ap, in_ap, reads, writes):
        if eng == "act":
            return self.S.add("act", lambda e: e.copy(out_ap, in_ap), reads, writes)
        return self.S.add(eng, lambda e: e.tensor_copy(out_ap, in_ap), reads, writes)

    def memset(self, eng, ap, val, writes):
        return self.S.add(eng, lambda e: e.memset(ap, val), (), writes)

    def load_weight(self, wbuf, w_dram, K, N, stage_bufs, col_split, cast_engs=("pool", "dve", "act")):
        KC = K // 128
        cw = N // col_split
        wv = self.v(wbuf, BF16, "p (k n) -> p k n", k=KC)
        i = 0
        for k in range(KC):
            for c in range(col_split):
                st = stage_bufs[i % len(stage_bufs)]
                stv = self.v(st)[:, 0:cw]
                self.load(stv, w_dram[k * 128:(k + 1) * 128, c * cw:(c + 1) * cw], (), (st,))
                ce = cast_engs[i % len(cast_engs)]
                self.copy(ce, wv[:, k, c * cw:(c + 1) * cw], stv, (st,), (wbuf,))
                i += 1

    def build(self):
        nc = self.nc
        T = self.T
        L = self.depth
        NB = self.NB
        self.xT = self.dram_in("xT", [D, T], F32)
        self.cT = self.dram_in("cT", [128, 8], F32)
        self.pos = self.dram_in("pos", [1, T], I32)
        self.pos2 = self.dram_in("pos2", [1, T], I32)
        self.cbf = self.dram_in("cbf", [128, 384], BF16)
        self.cf32 = self.dram_in("cf32", [128, 4], F32)
        self.ada_w = self.dram_in("ada_w", [L, D, 6 * D], F32)
        self.ada_b = self.dram_in("ada_b", [128, L, 48], F32)
        self.n1g = self.dram_in("n1g", [128, L, 8], F32)
        self.n2g = self.dram_in("n2g", [128, L, 8], F32)
        self.fng = self.dram_in("fng", [128, 8], F32)
        self.w_in = self.dram_in("w_in", [L, D, NIN], F32)
        self.lamv = self.dram_in("lamv", [1, L * 4 * 64], F32)
        self.sublng = self.dram_in("sublng", [1, L * 128], F32)
        self.w_ao = self.dram_in("w_ao", [L, 512, D], F32)
        self.cw = self.dram_in("cw", [128, L, 4, CK], F32)
        self.cvec = self.dram_in("cvec", [128, L, 3, 4], F32)
        self.w_co = self.dram_in("w_co", [L, 512, D], F32)
        self.w_out = self.dram_in("w_out", [L, D, D], F32)
        self.w_f1 = self.dram_in("w_f1", [L, D, 2 * DFF], F32)
        self.w_f2 = self.dram_in("w_f2", [L, DFF, D], F32)
        self.TY = T // 2 if self.split else T
        self.yT = nc.dram_tensor("yT", [D, self.TY], F32, kind="ExternalOutput").ap()
        self.XL = self.dram_scratch("XL", [D, T], F32)
        self.CSd2 = self.dram_scratch("CSd2", [2, 128, T], F32)

        self.XS = self.dram_scratch("XS", [D, T], F32)
        self.QKB = self.dram_scratch("QKB", [D, T], BF16)
        self.Vd = self.dram_scratch("Vd", [T, 512], BF16)
        self.Ud = self.dram_scratch("Ud", [512, T], BF16)
        self.Gd = self.dram_scratch("Gd", [2 * D, T], F32)
        self.UCd = self.dram_scratch("UCd", [512, T], BF16)
        self.OTd = self.dram_scratch("OTd", [512, T], BF16)
        self.CSd = self.dram_scratch("CSd", [2, 128, T], F32)
        self.MODd = self.dram_scratch("MODd", [128, L * 48], F32)

        with ExitStack() as es:
            arena_t = es.enter_context(nc.sbuf_tensor("arena", [128, self.sbuf_words], F32))
            psum_t = es.enter_context(nc.psum_tensor("psum", [128, 4096], F32))
            self.arena_ap = arena_t[:, :] if not hasattr(arena_t, "ap") else arena_t.ap()
            self.psum_ap = psum_t[:, :] if not hasattr(psum_t, "ap") else psum_t.ap()
            self.PA = Arena(self.sbuf_words)
            self.PS = Arena(4096, excl=True)

            self.phase_setup()
            stages = ["p1", "conv", "attn", "p3a", "p3b"]
            done = False
            for l in range(L):
                for st in (["blend"] if self.is_split(l) else []) + stages:
                    getattr(self, "phase_" + st)(l)
                    if self.stop_after == (l, st):
                        done = True
                        break
                if done:
                    break

            sems = []

            def new_sem():
                s = es.enter_context(nc.semaphore(f"s{len(sems)}"))
                sems.append(s)
                return s

            self.finalize()
            self.S.number(new_sem)
            self.n_sems = len(sems)
            with nc.Block() as block:
                @block.tensor
                def _(e):
                    self.S.emit("pe", e)

                @block.scalar
                def _(e):
                    self.S.emit("act", e)

                @block.vector
                def _(e):
                    self.S.emit("dve", e)

                @block.gpsimd
                def _(e):
                    self.S.emit("pool", e)

                @block.sync
                def _(e):
                    self.S.emit("sp", e)
        return nc

    def finalize(self):
        outs = [b for (name, blk), b in self.dram_bufs.items() if name == "yT" or name in self.debug]
        fin = self.PA.alloc("fin", 8)
        self.S.add("pool", lambda e: e.memset(self.v(fin)[:, 0:1], 0.0), outs, (fin,))

    def new_phase(self):
        self.PA.reset()
        self.PS.reset()

    def phase_setup(self):
        L = self.depth
        T = self.T
        A = self.PA
        self.b_cbf = A.alloc("cbf", 192)
        self.b_ones = A.alloc("ones", 64)
        self.b_cf = A.alloc("cf32", 8)
        self.b_mod = A.alloc("mod", L * 48)
        self.b_der = A.alloc("der", L * 48)
        self.b_n1g = A.alloc("n1g", L * 8)
        self.b_n2g = A.alloc("n2g", L * 8)
        self.b_fng = A.alloc("fng", 8)
        self.b_cw = A.alloc("cw", L * 4 * CK)
        self.b_cvec = A.alloc("cvec", L * 12)
        self.b_gfac = A.alloc("gfac", L * 128)
        self.b_nlam = A.alloc("nlam", 8)
        self.b_misc = A.alloc("misc", 8)
        A.base = A.top
        v = self.v
        ld = self.load
        ld(v(self.b_cbf, BF16), self.cbf[:, :], (), (self.b_cbf,))
        ld(v(self.b_cf)[:, 0:4], self.cf32[:, :], (), (self.b_cf,))
        ld(v(self.b_n1g)[:, 0:L * 8], self.n1g.rearrange("p l k -> p (l k)"), (), (self.b_n1g,))
        ld(v(self.b_n2g)[:, 0:L * 8], self.n2g.rearrange("p l k -> p (l k)"), (), (self.b_n2g,))
        ld(v(self.b_fng)[:, 0:8], self.fng[:, :], (), (self.b_fng,))
        ld(v(self.b_cw)[:, 0:L * 4 * CK], self.cw.rearrange("p l c j -> p (l c j)"), (), (self.b_cw,))
        ld(v(self.b_cvec)[:, 0:L * 12], self.cvec.rearrange("p l a c -> p (l a c)"), (), (self.b_cvec,))
        self.memset("dve", v(self.b_ones, BF16), 1.0, (self.b_ones,))
        self.memset("dve", v(self.b_misc)[:, 0:1], -PI, (self.b_misc,))
        self.ts("dve", v(self.b_misc)[:, 1:2], v(self.b_cf)[:, 3:4], 30000.0, -30000.0, ALU.mult, ALU.add, (self.b_cf,), (self.b_misc,))
        self.ident = v(self.b_cbf, BF16)[:, 0:128]
        self.tri = v(self.b_cbf, BF16)[:, 128:256]
        self.perm = v(self.b_cbf, BF16)[:, 256:384]
        self.ones = v(self.b_ones, BF16)

        lv = A.alloc("lamv", L * 256)
        ld(v(lv)[:, 0:L * 256], self.lamv.partition_broadcast(128), (), (lv,))
        ld(v(self.b_gfac)[:, 0:L * 128], self.sublng.partition_broadcast(128), (), (self.b_gfac,))
        lt = A.alloc("lt", 64)
        ls = A.alloc("ls", 8)
        lvv = v(lv, F32, "p (l a d) -> p l a d", l=L, a=4)
        for l in range(L):
            lam_init = 0.8 - 0.6 * math.exp(-0.3 * l)
            for j in range(2):
                self.tt("dve", v(lt)[:, 0:64], lvv[:, l, 2 * j, :], lvv[:, l, 2 * j + 1, :], ALU.mult, (lv,), (lt,))
                self.S.add("dve", lambda e, j=j: e.tensor_reduce(v(ls)[:, j:j + 1], v(lt)[:, 0:64], AX.X, ALU.add),
                           (lt,), (ls,))
            self.act(v(ls)[:, 2:4], v(ls)[:, 0:2], AF.Exp, (ls,), (ls,))
            self.tt("dve", v(ls)[:, 4:5], v(ls)[:, 3:4], v(ls)[:, 2:3], ALU.subtract, (ls,), (ls,))
            self.ts("dve", v(self.b_nlam)[:, l:l + 1], v(ls)[:, 4:5], -lam_init, None, ALU.add, None, (ls,), (self.b_nlam,))
            g = v(self.b_gfac)[:, l * 128:(l + 1) * 128]
            self.ts("dve", g, g, 1.0 - lam_init, None, ALU.mult, None, (self.b_gfac,), (self.b_gfac,))

        cact = A.alloc("cact", 8)
        ld(v(cact)[:, 0:8], self.cT[:, :], (), (cact,))
        self.act(v(cact)[:, 0:8], v(cact)[:, 0:8], AF.Silu, (cact,), (cact,))
        ld(v(self.b_mod)[:, 0:L * 48], self.ada_b.rearrange("p l j -> p (l j)"), (), (self.b_mod,))
        NG = 6
        stg = [A.alloc(f"adast{i}", 8 * 1024) for i in range(2)]
        pm = self.PS.alloc("pmod", 512)
        i = 0
        for l in range(L):
            for g in range(NG):
                st = stg[i % 2]
                i += 1
                stv = v(st, F32, "p (k n) -> p k n", k=8)
                for k in range(8):
                    ld(stv[:, k, :], self.ada_w[l, k * 128:(k + 1) * 128, g * 1024:(g + 1) * 1024], (), (st,))
                for jj in range(8):
                    j = g * 8 + jj
                    col = l * 48 + j
                    for k in range(8):
                        self.mm(self.pv(pm)[:, col:col + 1], stv[:, k, jj * 128:(jj + 1) * 128], v(cact)[:, k:k + 1],
                                k == 0, k == 7, (st, cact), (pm,))
        self.tt("dve", v(self.b_mod)[:, 0:L * 48], v(self.b_mod)[:, 0:L * 48], self.pv(pm)[:, 0:L * 48], ALU.add,
                (self.b_mod, pm), (self.b_mod,))
        modv = v(self.b_mod, F32, "p (l s k) -> p l s k", l=L, s=6)
        derv = v(self.b_der, F32, "p (l s k) -> p l s k", l=L, s=6)
        n1 = v(self.b_n1g, F32, "p (l k) -> p l k", l=L)
        n2 = v(self.b_n2g, F32, "p (l k) -> p l k", l=L)
        for l in range(L):
            self.stt(derv[:, l, 0, :], modv[:, l, 1, :], 1.0, n1[:, l, :], ALU.add, ALU.mult, (self.b_mod, self.b_n1g), (self.b_der,))
            self.copy("dve", derv[:, l, 1, :], modv[:, l, 0, :], (self.b_mod,), (self.b_der,))
            self.ts("dve", derv[:, l, 2, :], modv[:, l, 2, :], 1.0, None, ALU.add, None, (self.b_mod,), (self.b_der,))
            self.stt(derv[:, l, 3, :], modv[:, l, 4, :], 1.0, n2[:, l, :], ALU.add, ALU.mult, (self.b_mod, self.b_n2g), (self.b_der,))
            self.copy("dve", derv[:, l, 4, :], modv[:, l, 3, :], (self.b_mod,), (self.b_der,))
            self.ts("dve", derv[:, l, 5, :], modv[:, l, 5, :], 1.0, None, ALU.add, None, (self.b_mod,), (self.b_der,))
        self.derv = derv
        if "MODd" in self.debug:
            self.store(self.MODd[:, :], v(self.b_mod)[:, 0:L * 48], (self.b_mod,), (self.db("MODd"),))

        CH = 2048 if T >= 2048 else T
        pi_b = A.alloc("posi", CH)
        ang = A.alloc("ang", CH)
        t1 = A.alloc("t1", CH)
        t2 = A.alloc("t2", CH)
        res = A.alloc("res", CH)
        invf = v(self.b_cf)[:, 0:1]
        nss = v(self.b_cf)[:, 1:2]
        negpi = v(self.b_misc)[:, 0:1]
        tabs = [(self.pos, self.CSd, "CSd")] + ([(self.pos2, self.CSd2, "CSd2")] if self.split else [])
        for (posap, csd, csname) in tabs:
          for c0 in range(0, T, CH):
              ld(v(pi_b).bitcast(I32)[:, 0:CH], posap[:, c0:c0 + CH].partition_broadcast(128), (), (pi_b,))
              self.copy("dve", v(ang)[:, 0:CH], v(pi_b).bitcast(I32)[:, 0:CH], (pi_b,), (ang,))
              self.ts("dve", v(ang)[:, 0:CH], v(ang)[:, 0:CH], invf, None, ALU.mult, None, (ang, self.b_cf), (ang,))
              for which in range(2):
                  a = v(t1)[:, 0:CH]
                  b = v(t2)[:, 0:CH]
                  off = PI / 2 if which == 0 else 0.0
                  self.ts("dve", a, v(ang)[:, 0:CH], off, 1.0 / (2 * PI), ALU.add, ALU.mult, (ang,), (t1,))
                  self.copy("dve", b.bitcast(I32), a, (t1,), (t2,))
                  self.copy("dve", a, b.bitcast(I32), (t2,), (t1,))
                  self.ts("dve", b, v(ang)[:, 0:CH], off, None, ALU.add, None, (ang,), (t2,))
                  self.stt(b, a, -2 * PI, b, ALU.mult, ALU.add, (t1, t2), (t2,))
                  self.ts("dve", a, b, -PI, 2 * PI, ALU.is_lt, ALU.mult, (t2,), (t1,))
                  self.tt("dve", b, b, a, ALU.add, (t1, t2), (t2,))
                  self.ts("dve", a, b, PI, -2 * PI, ALU.is_gt, ALU.mult, (t2,), (t1,))
                  self.tt("dve", b, b, a, ALU.add, (t1, t2), (t2,))
                  self.ts("dve", b, b, PI, -PI, ALU.min, ALU.max, (t2,), (t2,))
                  self.act(v(res)[:, 0:CH], b, AF.Sin, (t2,), (res,))
                  if which == 1:
                      self.ts("dve", v(res)[:, 0:CH], v(res)[:, 0:CH], nss, None, ALU.mult, None, (res, self.b_cf), (res,))
                  self.store(csd[which, :, c0:c0 + CH], v(res)[:, 0:CH], (res,), (self.db(csname, (which, c0)),))
        self.cs_ch = CH

    def phase_blend(self, l):
        self.new_phase()
        A, v = self.PA, self.v
        NBo = self.NB // 2
        b0 = v(self.b_cf)[:, 2:3]
        b1 = v(self.b_cf)[:, 3:4]
        xa = [A.alloc(f"xa{i}", 8 * TB) for i in range(2)]
        xb = [A.alloc(f"xb{i}", 8 * TB) for i in range(2)]
        ta = [A.alloc(f"bta{i}", 8 * TB) for i in range(2)]
        tb = [A.alloc(f"btb{i}", 8 * TB) for i in range(2)]
        xsd = self.XS.rearrange("(k p) t -> p k t", p=128)
        xld = self.XL.rearrange("(k p) t -> p k t", p=128)
        pat = "p (k t) -> p k t"
        for s_ in range(NBo):
            a_ = xa[s_ % 2]; b_ = xb[s_ % 2]; t_a = ta[s_ % 2]; t_b = tb[s_ % 2]
            ga, gb = 2 * s_, 2 * s_ + 1
            self.load(v(a_, F32, pat, k=8), xsd[:, :, ga * TB:(ga + 1) * TB], (self.db("XS", 2 * ga), self.db("XS", 2 * ga + 1)), (a_,))
            self.load(v(b_, F32, pat, k=8), xsd[:, :, gb * TB:(gb + 1) * TB], (self.db("XS", 2 * gb), self.db("XS", 2 * gb + 1)), (b_,))
            self.act(v(t_a), v(b_), AF.Identity, (b_, self.b_cf), (t_a,), scale=b1)
            self.act(v(t_b), v(b_), AF.Identity, (b_, self.b_cf), (t_b,), scale=b0)
            self.stt(v(t_a), v(a_), b0, v(t_a), ALU.mult, ALU.add, (a_, t_a, self.b_cf), (t_a,))
            self.stt(v(t_b), v(a_), b1, v(t_b), ALU.mult, ALU.add, (a_, t_b, self.b_cf), (t_b,))
            lo, lp = s_, NBo + s_
            self.store(xld[:, :, lo * TB:(lo + 1) * TB], v(t_a, F32, pat, k=8), (t_a,), (self.db("XL", 2 * lo), self.db("XL", 2 * lo + 1)))
            self.store(xld[:, :, lp * TB:(lp + 1) * TB], v(t_b, F32, pat, k=8), (t_b,), (self.db("XL", 2 * lp), self.db("XL", 2 * lp + 1)))

    def rmsnorm_block(self, xb, hb, sqb, rsb, ps_stat, Acol, Bcol, tmps, W=TB):
        v = self.v
        xv = v(xb, F32, "p (k t) -> p k t", k=8)
        hv = v(hb, BF16, "p (k t) -> p k t", k=8)
        sqv = v(sqb, BF16, "p (k t) -> p k t", k=8)
        for k in range(8):
            if k % 2 == 0:
                self.act(sqv[:, k, :], xv[:, k, :], AF.Square, (xb,), (sqb,))
            else:
                self.tt("pool", sqv[:, k, :], xv[:, k, :], xv[:, k, :], ALU.mult, (xb,), (sqb,))
        for k in range(8):
            self.mm(self.pv(ps_stat)[:, 0:W], self.ones, sqv[:, k, :], k == 0, k == 7, (sqb, self.b_ones), (ps_stat,))
        rs = v(rsb)[:, 0:W]
        self.ts("dve", rs, self.pv(ps_stat)[:, 0:W], 1.0 / D, 1e-6, ALU.mult, ALU.add, (ps_stat,), (rsb,))
        self.act(rs, rs, AF.Sqrt, (rsb,), (rsb,))
        self.S.add("dve", lambda e: e.reciprocal(rs, rs), (rsb,), (rsb,))
        for k in range(8):
            tb = tmps[k % len(tmps)]
            tv = v(tb)[:, 0:W]
            self.stt(tv, xv[:, k, :], Acol(k), rs, ALU.mult, ALU.mult, (xb, rsb, self.b_der), (tb,))
            self.act(hv[:, k, :], tv, AF.Identity, (tb, self.b_der), (hb,), bias=Bcol(k))

    def phase_p1(self, l):
        self.new_phase()
        A, PS, v, pv = self.PA, self.PS, self.v, self.pv
        T, NB = self.T, self.NB
        sp = self.is_split(l)
        xin = self.xT if l == 0 else (self.XL if sp else self.XS)
        xname = "xT" if l == 0 else ("XL" if sp else "XS")
        csd, csname = (self.CSd2, "CSd2") if sp else (self.CSd, "CSd")
        NBo = NB // 2 if sp else NB
        W = A.alloc("w_in", 8 * NIN // 2)
        stg = [A.alloc(f"wst{i}", 1536) for i in range(2)]
        self.load_weight(W, self.w_in[l], D, NIN, stg, 3)
        Wv = v(W, BF16, "p (k n) -> p k n", k=8)
        xb = [A.alloc(f"x{i}", 8 * TB) for i in range(2)]
        hbs = [A.alloc(f"h{i}", 8 * TB // 2) for i in range(2)]
        sqb = A.alloc("sq", 8 * TB // 2)
        tmps = [A.alloc(f"tmp{i}", TB) for i in range(2)]
        rsb = A.alloc("rs", TB)
        ostg = [A.alloc(f"o{i}", TB) for i in range(6)]
        sgb = [A.alloc(f"sg{i}", TB) for i in range(2)]
        Cb = [A.alloc(f"C{i}", TB) for i in range(2)]
        Sb = [A.alloc(f"S{i}", TB) for i in range(2)]
        zb = [A.alloc(f"zb{i}", TB // 2) for i in range(2)]
        r1 = [A.alloc(f"r1{i}", TB) for i in range(2)]
        r2 = [A.alloc(f"r2{i}", TB) for i in range(2)]
        pstat = PS.alloc("pstat", 512)
        pb = [PS.alloc(f"pb{i}", 512) for i in range(7)]
        derv = self.derv
        st = {"oi": 0, "pi": 0, "n": 0}
        xv_d = xin.rearrange("(k p) t -> p k t", p=128)
        CH = self.cs_ch

        def load_x(i):
            x = xb[i % 2]
            self.load(v(x, F32, "p (k t) -> p k t", k=8), xv_d[:, :, i * TB:(i + 1) * TB],
                      (self.db(xname, 2 * i), self.db(xname, 2 * i + 1)), (x,))

        def norm(i):
            self.rmsnorm_block(xb[i % 2], hbs[i % 2], sqb, rsb, pstat, lambda k: derv[:, l, 0, k:k + 1],
                               lambda k: derv[:, l, 1, k:k + 1], tmps)

        def nextp():
            p = pb[st["pi"] % 7]; st["pi"] += 1
            return p

        def nexto():
            o = ostg[st["oi"] % 6]; st["oi"] += 1
            return o

        load_x(0)
        if NB > 1:
            load_x(1)
        norm(0)
        for i in range(NB):
            t0 = i * TB
            hb = hbs[i % 2]
            hv = v(hb, BF16, "p (k t) -> p k t", k=8)
            C = Cb[i % 2]; Sx = Sb[i % 2]
            c0 = (t0 // CH) * CH
            self.load(v(C)[:, 0:TB], csd[0, :, t0:t0 + TB], (self.db(csname, (0, c0)),), (C,))
            self.load(v(Sx)[:, 0:TB], csd[1, :, t0:t0 + TB], (self.db(csname, (1, c0)),), (Sx,))
            full = i < NBo

            def proj(m, pbuf):
                for k in range(8):
                    self.mm(pv(pbuf), Wv[:, k, m * 128:(m + 1) * 128], hv[:, k, :], k == 0, k == 7, (W, hb), (pbuf,))

            for m in (range(8) if full else range(4, 8)):
                p = nextp()
                proj(m, p)
                n = st["n"]; st["n"] += 1
                z = zb[n % 2]; a1 = r1[n % 2]; a2 = r2[n % 2]
                zv = v(z, BF16)[:, 0:TB]
                self.copy("dve", zv, pv(p), (p,), (z,))
                p2 = nextp()
                self.mm(pv(p2), self.perm, zv, True, True, (z, self.b_cbf), (p2,))
                self.tt("dve", v(a1)[:, 0:TB], pv(p), v(C)[:, 0:TB], ALU.mult, (p, C, z), (a1,))
                self.tt("dve", v(a2)[:, 0:TB], pv(p2), v(Sx)[:, 0:TB], ALU.mult, (p2, Sx), (a2,))
                o = nexto()
                ov = v(o, BF16)[:, 0:TB]
                self.tt("pool", ov, v(a1)[:, 0:TB], v(a2)[:, 0:TB], ALU.add, (a1, a2), (o,))
                self.store(self.QKB[m * 128:(m + 1) * 128, t0:t0 + TB], ov, (o,), (self.db("QKB", (m, i)),))
            if i + 1 < NB:
                norm(i + 1)
            if i + 2 < NB:
                load_x(i + 2)
            for s_ in range(4):
                p = nextp()
                for k in range(8):
                    self.mm(pv(p), hv[:, k, s_ * 128:(s_ + 1) * 128], Wv[:, k, 1024:1536], k == 0, k == 7, (W, hb), (p,))
                o = nexto()
                ov = v(o, BF16)[:, 0:512]
                self.copy("dve" if s_ % 2 == 0 else "act", ov, pv(p), (p,), (o,))
                self.store(self.Vd[t0 + s_ * 128:t0 + (s_ + 1) * 128, :], ov, (o,), (self.db("Vd", i),))
            for c in range(4):
                p1 = nextp()
                proj(16 + c, p1)
                sg = sgb[c % 2]
                self.act(v(sg)[:, 0:TB], pv(p1), AF.Sigmoid, (p1,), (sg,))
                p2 = nextp()
                proj(12 + c, p2)
                o = nexto()
                ov = v(o, BF16)[:, 0:512]
                self.tt("dve", ov, pv(p2), v(sg)[:, 0:TB], ALU.mult, (p2, sg), (o,))
                self.store(self.Ud[c * 128:(c + 1) * 128, t0:t0 + TB], ov, (o,), (self.db("Ud", (c, i)),))
            for m in (range(16) if full else ()):
                p = nextp()
                proj(20 + m, p)
                o = nexto()
                self.act(v(o)[:, 0:TB], pv(p), AF.Sigmoid, (p,), (o,))
                self.store(self.Gd[m * 128:(m + 1) * 128, t0:t0 + TB], v(o)[:, 0:TB], (o,), (self.db("Gd", (m, i)),))

    def phase_conv(self, l):
        self.new_phase()
        A, PS, v, pv = self.PA, self.PS, self.v, self.pv
        NB = self.NB
        dg = A.alloc("diag", 4 * CK * 64)
        dgv = v(dg, BF16, "p (c j n) -> p c j n", c=4, j=CK)
        cwv = v(self.b_cw)[:, 0:self.depth * 4 * CK].rearrange("p (l c j) -> p l c j", l=self.depth, c=4)
        for c in range(4):
            for j in range(CK):
                self.ts("dve" if (c * CK + j) % 2 == 0 else "pool", dgv[:, c, j, :], self.ident, cwv[:, l, c, j:j + 1], None, ALU.mult, None,
                        (self.b_cbf, self.b_cw), (dg,))
        HW = TB + 32
        ub = [A.alloc(f"u{i}", 4 * HW // 2) for i in range(2)]
        yb = A.alloc("y", 4 * TB)
        ybf = A.alloc("ybf", 4 * TB // 2)
        ysq = A.alloc("ysq", 4 * TB // 2)
        mean = A.alloc("mean", TB)
        rstd = A.alloc("rstd", TB)
        t1 = [A.alloc(f"ct{i}", TB) for i in range(2)]
        ob = [A.alloc(f"co{i}", TB // 2) for i in range(3)]
        pc = [PS.alloc(f"pc{i}", 512) for i in range(4)]
        psum_s = PS.alloc("pss", 512)
        psum_q = PS.alloc("psq", 512)
        cv = v(self.b_cvec)[:, 0:self.depth * 12].rearrange("p (l a c) -> p l a c", l=self.depth, a=3)
        n = 0
        sp = self.is_split(l)
        NBo = NB // 2 if sp else NB
        if sp:
            hA = [A.alloc(f"hA{i}", 64) for i in range(2)]
            hB = [A.alloc(f"hB{i}", 64) for i in range(2)]
            b0 = v(self.b_cf)[:, 2:3]
            b1 = v(self.b_cf)[:, 3:4]
        for i in range(NBo):
            t0 = i * TB
            u = ub[i % 2]
            uv = v(u, BF16, "p (c t) -> p c t", c=4)
            ud = self.Ud.rearrange("(c p) t -> p c t", p=128)
            if sp:
                ha = hA[i % 2]; hb_ = hB[i % 2]
                hav = v(ha, BF16, "p (c t) -> p c t", c=4)
                hbv = v(hb_, BF16, "p (c t) -> p c t", c=4)
                la = NBo + i
                self.load(hav, ud[:, :, (la + 1) * TB - 32:(la + 1) * TB], tuple(self.db("Ud", (c, la)) for c in range(4)), (ha,))
                if i == 0:
                    self.memset("pool", hbv, 0.0, (hb_,))
                else:
                    lb_ = NBo + i - 1
                    self.load(hbv, ud[:, :, (lb_ + 1) * TB - 32:(lb_ + 1) * TB], tuple(self.db("Ud", (c, lb_)) for c in range(4)), (hb_,))
                self.ts("dve", hav, hav, b1, None, ALU.mult, None, (ha, self.b_cf), (ha,))
                self.stt(uv[:, :, 0:32], hbv, b0, hav, ALU.mult, ALU.add, (hb_, ha, self.b_cf), (u,))
                self.load(uv[:, :, 32:32 + TB], ud[:, :, t0:t0 + TB], tuple(self.db("Ud", (c, i)) for c in range(4)), (u,))
            elif i == 0:
                self.memset("pool", uv[:, :, 0:32], 0.0, (u,))
                self.load(uv[:, :, 32:32 + TB], ud[:, :, 0:TB], tuple(self.db("Ud", (c, 0)) for c in range(4)), (u,))
            else:
                self.load(uv[:, :, 0:HW], ud[:, :, t0 - 32:t0 + TB],
                          tuple(self.db("Ud", (c, ii)) for c in range(4) for ii in (i - 1, i)), (u,))
            yv = v(yb, F32, "p (c t) -> p c t", c=4)
            ybv = v(ybf, BF16, "p (c t) -> p c t", c=4)
            ysv = v(ysq, BF16, "p (c t) -> p c t", c=4)
            for c in range(4):
                p = pc[c]
                for j in range(CK):
                    self.mm(pv(p), dgv[:, c, j, :], uv[:, c, 2 + j:2 + j + TB], j == 0, j == CK - 1, (dg, u), (p,))
                self.act(yv[:, c, :], pv(p), AF.Identity, (p, self.b_cvec), (yb,), bias=cv[:, l, 0, c:c + 1])
                self.copy("pool", ybv[:, c, :], yv[:, c, :], (yb,), (ybf,))
                self.act(ysv[:, c, :], yv[:, c, :], AF.Square, (yb,), (ysq,))
            for c in range(4):
                self.mm(pv(psum_s), self.ones, ybv[:, c, :], c == 0, c == 3, (ybf, self.b_ones), (psum_s,))
            for c in range(4):
                self.mm(pv(psum_q), self.ones, ysv[:, c, :], c == 0, c == 3, (ysq, self.b_ones), (psum_q,))
            mv = v(mean)[:, 0:TB]
            rv = v(rstd)[:, 0:TB]
            self.ts("dve", mv, pv(psum_s), 1.0 / 512, None, ALU.mult, None, (psum_s,), (mean,))
            self.tt("dve", rv, mv, mv, ALU.mult, (mean,), (rstd,))
            self.stt(rv, pv(psum_q), 1.0 / 512, rv, ALU.mult, ALU.subtract, (psum_q, rstd), (rstd,))
            self.ts("dve", rv, rv, 1e-5, None, ALU.add, None, (rstd,), (rstd,))
            self.act(rv, rv, AF.Sqrt, (rstd,), (rstd,))
            self.S.add("dve", lambda e, rv=rv: e.reciprocal(rv, rv), (rstd,), (rstd,))
            for c in range(4):
                ta = t1[n % 2]; o = ob[n % 3]; n += 1
                tav = v(ta)[:, 0:TB]
                self.tt("dve", tav, yv[:, c, :], mv, ALU.subtract, (yb, mean), (ta,))
                self.tt("pool", tav, tav, rv, ALU.mult, (ta, rstd), (ta,))
                ov = v(o, BF16)[:, 0:TB]
                self.act(ov, tav, AF.Silu, (ta, self.b_cvec), (o,), scale=cv[:, l, 1, c:c + 1], bias=cv[:, l, 2, c:c + 1])
                self.store(self.UCd[c * 128:(c + 1) * 128, t0:t0 + TB], ov, (o,), (self.db("UCd", (c, i)),))

    def phase_attn(self, l):
        self.new_phase()
        A, PS, v, pv = self.PA, self.PS, self.v, self.pv
        T, NB = self.T, self.NB
        NT = T // 128
        scale = 1.0 / 8.0
        QTb = [A.alloc(f"QT{i}", T // 2) for i in range(2)]
        KTb = [A.alloc(f"KT{i}", T // 2) for i in range(2)]
        V1b = [A.alloc(f"V1{i}", NT * 130 // 2) for i in range(2)]
        ptb = [[A.alloc(f"pt{c}{i}", TB // 2) for i in range(3)] for c in range(2)]
        osb = [A.alloc(f"os{i}", 128) for i in range(2)]
        t2b = [A.alloc(f"t2{i}", 128) for i in range(2)]
        sm = [A.alloc(f"sm{i}", 8) for i in range(2)]
        oT = [A.alloc(f"oT{i}", TB // 2) for i in range(2)]
        psS = [[PS.alloc(f"pS{c}{i}", 512) for i in range(2)] for c in range(2)]
        accbank = [PS.alloc(f"acc{i}", 512) for i in range(3)]
        ptr = PS.alloc("ptr", 512)
        gfac = v(self.b_gfac)[:, l * 128:(l + 1) * 128]
        nlam = v(self.b_nlam)[:, l:l + 1]
        st = {"nS": 0, "nE": 0}

        def acc(c, s):
            idx = c * 4 + s
            bk = accbank[idx // 3]
            off = (idx % 3) * 160
            return bk, pv(bk)[:, off:off + 129]

        def load_head(h):
            QT = QTb[h % 2]; KT = KTb[h % 2]; V1 = V1b[h % 2]
            QTv = v(QT, BF16)[:, 0:T]
            KTv = v(KT, BF16)[:, 0:T]
            V1v = v(V1, BF16, "p (n d) -> p n d", d=130)
            self.load(QTv, self.QKB[h * 128:(h + 1) * 128, :], tuple(self.db("QKB", (h, i)) for i in range(NB)), (QT,))
            self.load(KTv, self.QKB[512 + h * 128:512 + (h + 1) * 128, :], tuple(self.db("QKB", (4 + h, i)) for i in range(NB)), (KT,))
            vdv = self.Vd.rearrange("(n p) d -> p n d", p=128)
            for n0 in range(0, NT, 8):
                self.load(V1v[:, n0:n0 + 8, 0:128], vdv[:, n0:n0 + 8, h * 128:(h + 1) * 128],
                          tuple(self.db("Vd", i) for i in range(n0 // 4, n0 // 4 + 2)), (V1,))
            self.memset("pool", V1v[:, :, 128:130], 1.0, (V1,))

        sp = self.is_split(l)
        NBo = NB // 2 if sp else NB
        mbias = v(self.b_misc)[:, 1:2]
        accs = [[[A.alloc(f"accs{i}{c}{s_}", 136) for s_ in range(4)] for c in range(2)] for i in range(2)]
        onb = [[A.alloc(f"onb{i}{s_}", 64) for s_ in range(4)] for i in range(2)]

        def block_iters(i):
            its = [(kt, "c", kt - 4 * i) for kt in range(4 * i + 4)]
            if sp:
                base = 4 * NBo
                its += [(base + kt, "f", -1) for kt in range(4 * i)]
                its += [(base + 4 * i + kt, "m", -1) for kt in range(4)]
            return its

        load_head(0)
        for h in range(NH):
            if h + 1 < NH:
                load_head(h + 1)
            QT = QTb[h % 2]; KT = KTb[h % 2]; V1 = V1b[h % 2]
            QTv = v(QT, BF16)[:, 0:T]
            KTv = v(KT, BF16)[:, 0:T]
            V1v = v(V1, BF16, "p (n d) -> p n d", d=130)

            def emit_qk(i, kt, mode, sd):
                q0 = i * TB
                f0 = max(sd, 0) * 128 if mode == "c" else 0
                nS = st["nS"]; st["nS"] += 1
                pts = []
                for c in range(2):
                    pS = psS[c][nS % 2]
                    r0 = c * 64
                    self.mm(pv(pS)[:, f0:TB], KTv[r0:r0 + 64, kt * 128:(kt + 1) * 128], QTv[r0:r0 + 64, q0 + f0:q0 + TB],
                            True, True, (KT, QT), (pS,))
                    pt = ptb[c][nS % 3]
                    ptv = v(pt, BF16)[:, 0:TB]
                    if mode == "m":
                        self.act(ptv[:, f0:TB], pv(pS)[:, f0:TB], AF.Exp, (pS, self.b_misc), (pt,), scale=scale, bias=mbias)
                    else:
                        self.act(ptv[:, f0:TB], pv(pS)[:, f0:TB], AF.Exp, (pS,), (pt,), scale=scale)
                    if mode == "c" and sd >= 0:
                        self.tt("pool", ptv[:, f0:f0 + 128], ptv[:, f0:f0 + 128], self.tri, ALU.mult, (pt, self.b_cbf), (pt,))
                    pts.append((pt, ptv))
                return pts, f0 // 128

            def emit_pv(kt, pts, smin):
                for s in range(smin, 4):
                    for c in range(2):
                        bk, av = acc(c, s)
                        pt, ptv = pts[c]
                        self.mm(av, ptv[:, s * 128:(s + 1) * 128], V1v[:, kt, 0:129], False, True,
                                (pt, V1), (bk,), skip_group_check=True)

            def epilogue_a(i):
                cps = []
                for s in range(4):
                    row = []
                    for c in range(2):
                        bk, av = acc(c, s)
                        ab = accs[i % 2][c][s]
                        abv = v(ab)[:, 0:129]
                        self.copy("dve", abv, av, (bk,), (ab,))
                        row.append((ab, abv))
                    cps.append(row)
                zero_acc()
                first = cps[0][0][0]
                for s in range(4):
                    (b1_, a1), (b2_, a2) = cps[s]
                    nE = st["nE"]; st["nE"] += 1
                    smb = sm[nE % 2]; os_ = osb[nE % 2]; t2 = t2b[nE % 2]; on = onb[i % 2][s]
                    smv = v(smb)
                    self.S.add("dve", lambda e, smv=smv, a1=a1: e.reciprocal(smv[:, 0:1], a1[:, 128:129]), (b1_,), (smb,))
                    self.S.add("dve", lambda e, smv=smv, a2=a2: e.reciprocal(smv[:, 1:2], a2[:, 128:129]), (b2_,), (smb,))
                    self.tt("dve", smv[:, 1:2], smv[:, 1:2], nlam, ALU.mult, (smb, self.b_nlam), (smb,))
                    t2v = v(t2)[:, 0:128]
                    osv = v(os_)[:, 0:128]
                    self.ts("dve", t2v, a2[:, 0:128], smv[:, 1:2], None, ALU.mult, None, (b2_, smb), (t2,))
                    self.stt(osv, a1[:, 0:128], smv[:, 0:1], t2v, ALU.mult, ALU.add, (b1_, smb, t2), (os_,))
                    self.tt("dve", t2v, osv, osv, ALU.mult, (os_,), (t2,))
                    self.S.add("dve", lambda e, smv=smv, t2v=t2v: e.tensor_reduce(smv[:, 2:3], t2v, AX.X, ALU.add), (t2,), (smb,))
                    self.ts("dve", smv[:, 2:3], smv[:, 2:3], 1.0 / 128, 1e-5, ALU.mult, ALU.add, (smb,), (smb,))
                    self.act(smv[:, 2:3], smv[:, 2:3], AF.Ln, (smb,), (smb,))
                    self.act(smv[:, 2:3], smv[:, 2:3], AF.Exp, (smb,), (smb,), scale=-0.5)
                    onv = v(on, BF16)[:, 0:128]
                    self.stt(onv, osv, smv[:, 2:3], gfac, ALU.mult, ALU.mult, (os_, smb, self.b_gfac), (on,))
                return first

            def epilogue_b(i, drain=None):
                q0 = i * TB
                oTb = oT[i % 2]
                oTv = v(oTb, BF16)[:, 0:TB]
                for s in range(4):
                    on = onb[i % 2][s]
                    onv = v(on, BF16)[:, 0:128]
                    ptv_ = pv(ptr).bitcast(BF16)[:, s * 128:(s + 1) * 128]
                    rd = (on, self.b_cbf) + ((drain,) if (drain is not None and s == 0) else ())
                    self.S.add("pe", lambda e, ptv_=ptv_, onv=onv: e.transpose(ptv_, onv, self.ident), rd, (ptr,))
                self.copy("act", oTv, pv(ptr).bitcast(BF16)[:, 0:TB], (ptr,), (oTb,))
                self.store(self.OTd[h * 128:(h + 1) * 128, q0:q0 + TB], oTv, (oTb,), (self.db("OTd", (h, i)),))

            def zero_acc():
                for bk in accbank:
                    self.memset("dve", pv(bk), 0.0, (bk,))

            iters = []
            for i in range(NBo):
                bi = block_iters(i)
                for n_, (kt, mode, sd) in enumerate(bi):
                    iters.append((i, kt, mode, sd, n_ == len(bi) - 1))
            prev = None
            pending = []
            zero_acc()
            for (i, kt, mode, sd, lastk) in iters:
                pts, smin = emit_qk(i, kt, mode, sd)
                if prev is not None:
                    pi_, pk_, pp_, ps_, pl_ = prev
                    emit_pv(pk_, pp_, ps_)
                    if pl_:
                        dr = epilogue_a(pi_)
                        if DEFER_TR:
                            if pending:
                                epilogue_b(pending.pop(0), dr)
                            pending.append(pi_)
                        else:
                            epilogue_b(pi_)
                prev = (i, kt, pts, smin, lastk)
            pi_, pk_, pp_, ps_, pl_ = prev
            emit_pv(pk_, pp_, ps_)
            dr = epilogue_a(pi_)
            while pending:
                epilogue_b(pending.pop(0), dr)
            epilogue_b(pi_)

    def phase_p3a(self, l):
        self.new_phase()
        A, PS, v, pv = self.PA, self.PS, self.v, self.pv
        NB = self.NB
        derv = self.derv
        sp = self.is_split(l)
        xin = self.xT if l == 0 else (self.XL if sp else self.XS)
        xname = "xT" if l == 0 else ("XL" if sp else "XS")
        xout, xoname = (self.XL, "XL") if sp else (self.XS, "XS")
        if sp:
            NB = NB // 2
        Wa = A.alloc("w_ao", 4 * D // 2)
        Wc = A.alloc("w_co", 4 * D // 2)
        Wo = A.alloc("w_o", 8 * D // 2)
        stg = [A.alloc(f"wst{i}", 1024) for i in range(3)]
        self.load_weight(Wa, self.w_ao[l], 512, D, stg, 1)
        self.load_weight(Wc, self.w_co[l], 512, D, stg, 1)
        self.load_weight(Wo, self.w_out[l], D, D, stg, 1)
        Wav = v(Wa, BF16, "p (k n) -> p k n", k=4)
        Wcv = v(Wc, BF16, "p (k n) -> p k n", k=4)
        Wov = v(Wo, BF16, "p (k n) -> p k n", k=8)
        xb = [A.alloc(f"x{i}", 8 * TB) for i in range(2)]
        gb = [A.alloc(f"g{i}", 16 * TB) for i in range(2)]
        otb = [A.alloc(f"ot{i}", 4 * TB // 2) for i in range(2)]
        ucb = [A.alloc(f"uc{i}", 4 * TB // 2) for i in range(2)]
        mixb = A.alloc("mix", 8 * TB // 2)
        ta = [A.alloc(f"ta{i}", TB) for i in range(2)]
        tb_ = [A.alloc(f"tb{i}", TB) for i in range(2)]
        pa = [PS.alloc(f"pa{i}", 512) for i in range(2)]
        pc = [PS.alloc(f"pc{i}", 512) for i in range(2)]
        po = [PS.alloc(f"po{i}", 512) for i in range(2)]
        n = 0
        for i in range(NB):
            t0 = i * TB
            x = xb[i % 2]; g = gb[i % 2]; ot = otb[i % 2]; uc = ucb[i % 2]
            xv = v(x, F32, "p (k t) -> p k t", k=8)
            gv = v(g, F32, "p (k t) -> p k t", k=16)
            otv = v(ot, BF16, "p (k t) -> p k t", k=4)
            ucv = v(uc, BF16, "p (k t) -> p k t", k=4)
            self.load(otv, self.OTd.rearrange("(k p) t -> p k t", p=128)[:, :, t0:t0 + TB],
                      tuple(self.db("OTd", (h, i)) for h in range(4)), (ot,))
            self.load(ucv, self.UCd.rearrange("(k p) t -> p k t", p=128)[:, :, t0:t0 + TB],
                      tuple(self.db("UCd", (c, i)) for c in range(4)), (uc,))
            self.load(gv[:, 0:8, :], self.Gd.rearrange("(k p) t -> p k t", p=128)[:, 0:8, t0:t0 + TB],
                      tuple(self.db("Gd", (m, i)) for m in range(8)), (g,))
            self.load(gv[:, 8:16, :], self.Gd.rearrange("(k p) t -> p k t", p=128)[:, 8:16, t0:t0 + TB],
                      tuple(self.db("Gd", (m, i)) for m in range(8, 16)), (g,))
            self.load(xv, xin.rearrange("(k p) t -> p k t", p=128)[:, :, t0:t0 + TB], (self.db(xname, 2 * i), self.db(xname, 2 * i + 1)), (x,))
            mixv = v(mixb, BF16, "p (k t) -> p k t", k=8)
            for m in range(8):
                p1 = pa[m % 2]; p2 = pc[m % 2]
                for k in range(4):
                    self.mm(pv(p1), Wav[:, k, m * 128:(m + 1) * 128], otv[:, k, :], k == 0, k == 3, (Wa, ot), (p1,))
                for k in range(4):
                    self.mm(pv(p2), Wcv[:, k, m * 128:(m + 1) * 128], ucv[:, k, :], k == 0, k == 3, (Wc, uc), (p2,))
                a = ta[n % 2]; b = tb_[n % 2]; n += 1
                av = v(a)[:, 0:TB]; bv = v(b)[:, 0:TB]
                self.tt("dve", av, pv(p1), gv[:, m, :], ALU.mult, (p1, g), (a,))
                self.tt("dve", bv, pv(p2), gv[:, 8 + m, :], ALU.mult, (p2, g), (b,))
                self.tt("pool", mixv[:, m, :], av, bv, ALU.add, (a, b), (mixb,))
            for m in range(8):
                p = po[m % 2]
                for k in range(8):
                    self.mm(pv(p), Wov[:, k, m * 128:(m + 1) * 128], mixv[:, k, :], k == 0, k == 7, (Wo, mixb), (p,))
                self.stt(xv[:, m, :], pv(p), derv[:, l, 2, m:m + 1], xv[:, m, :], ALU.mult, ALU.add, (p, x, self.b_der), (x,))
            self.store(xout.rearrange("(k p) t -> p k t", p=128)[:, :, t0:t0 + TB], xv, (x,), (self.db(xoname, 2 * i), self.db(xoname, 2 * i + 1)))

    def phase_p3b(self, l):
        self.new_phase()
        A, PS, v, pv = self.PA, self.PS, self.v, self.pv
        FB = 256
        NBF = self.T // FB
        derv = self.derv
        last = (l == self.depth - 1)
        sp = self.is_split(l)
        xsrc, xsname = (self.XL, "XL") if sp else (self.XS, "XS")
        if sp:
            NBF = NBF // 2
        KF = DFF // 128
        W1 = A.alloc("w_f1", 8 * 2 * DFF // 2)
        W2 = A.alloc("w_f2", KF * D // 2)
        stg = [A.alloc(f"wst{i}", 704) for i in range(2)]
        self.load_weight(W1, self.w_f1[l], D, 2 * DFF, stg, 8)
        self.load_weight(W2, self.w_f2[l], DFF, D, stg, 2)
        W1v = v(W1, BF16, "p (k n) -> p k n", k=8)
        W2v = v(W2, BF16, "p (k n) -> p k n", k=KF)
        xb = [A.alloc(f"x{i}", 8 * FB) for i in range(2)]
        hbs = [A.alloc(f"h{i}", 8 * FB // 2) for i in range(2)]
        sqb = A.alloc("sq", 8 * FB // 2)
        tmps = [A.alloc(f"tmp{i}", FB) for i in range(2)]
        yb = A.alloc("yb", 8 * FB) if last else None
        rsb = A.alloc("rs", FB)
        gub = A.alloc("gu", KF * FB // 2)
        sgb = [A.alloc(f"sg{i}", FB) for i in range(2)]
        pstat = PS.alloc("pstat", 512)
        pg = [PS.alloc(f"pg{i}", 512) for i in range(2)]
        pu = [PS.alloc(f"pu{i}", 512) for i in range(2)]
        po = [PS.alloc(f"po{i}", 512) for i in range(2)]
        fng = v(self.b_fng)
        def load_x(i):
            x = xb[i % 2]
            self.load(v(x, F32, "p (k t) -> p k t", k=8), xsrc.rearrange("(k p) t -> p k t", p=128)[:, :, i * FB:(i + 1) * FB],
                      (self.db(xsname, i),), (x,))

        def norm(i):
            self.rmsnorm_block(xb[i % 2], hbs[i % 2], sqb, rsb, pstat, lambda k: derv[:, l, 3, k:k + 1],
                               lambda k: derv[:, l, 4, k:k + 1], tmps, W=FB)

        load_x(0)
        if NBF > 1:
            load_x(1)
        norm(0)
        for i in range(NBF):
            t0 = i * FB
            x = xb[i % 2]
            xv = v(x, F32, "p (k t) -> p k t", k=8)
            hb = hbs[i % 2]
            hv = v(hb, BF16, "p (k t) -> p k t", k=8)
            guv = v(gub, BF16, "p (k t) -> p k t", k=KF)
            for m in range(KF):
                if m == KF // 2 and i + 1 < NBF:
                    norm(i + 1)
                p1 = pg[m % 2]; p2 = pu[m % 2]; sg = sgb[m % 2]
                for k in range(8):
                    self.mm(pv(p1)[:, 0:FB], W1v[:, k, m * 128:(m + 1) * 128], hv[:, k, :], k == 0, k == 7, (W1, hb), (p1,))
                for k in range(8):
                    self.mm(pv(p2)[:, 0:FB], W1v[:, k, DFF + m * 128:DFF + (m + 1) * 128], hv[:, k, :], k == 0, k == 7, (W1, hb), (p2,))
                self.act(v(sg)[:, 0:FB], pv(p1)[:, 0:FB], AF.Silu, (p1,), (sg,))
                self.tt("dve", guv[:, m, :], pv(p2)[:, 0:FB], v(sg)[:, 0:FB], ALU.mult, (p2, sg), (gub,))
            for m in range(8):
                p = po[m % 2]
                for k in range(KF):
                    self.mm(pv(p)[:, 0:FB], W2v[:, k, m * 128:(m + 1) * 128], guv[:, k, :], k == 0, k == KF - 1, (W2, gub), (p,))
                self.stt(xv[:, m, :], pv(p)[:, 0:FB], derv[:, l, 5, m:m + 1], xv[:, m, :], ALU.mult, ALU.add, (p, x, self.b_der), (x,))
            if not last:
                self.store(self.XS.rearrange("(k p) t -> p k t", p=128)[:, :, t0:t0 + FB], xv, (x,), (self.db("XS", i),))
            else:
                sqv = v(sqb, BF16, "p (k t) -> p k t", k=8)
                for k in range(8):
                    if k % 2 == 0:
                        self.act(sqv[:, k, :], xv[:, k, :], AF.Square, (x,), (sqb,))
                    else:
                        self.tt("pool", sqv[:, k, :], xv[:, k, :], xv[:, k, :], ALU.mult, (x,), (sqb,))
                for k in range(8):
                    self.mm(pv(pstat)[:, 0:FB], self.ones, sqv[:, k, :], k == 0, k == 7, (sqb, self.b_ones), (pstat,))
                rs = v(rsb)[:, 0:FB]
                self.ts("dve", rs, pv(pstat)[:, 0:FB], 1.0 / D, 1e-6, ALU.mult, ALU.add, (pstat,), (rsb,))
                self.act(rs, rs, AF.Sqrt, (rsb,), (rsb,))
                self.S.add("dve", lambda e, rs=rs: e.reciprocal(rs, rs), (rsb,), (rsb,))
                yv = v(yb, F32, "p (k t) -> p k t", k=8)
                for k in range(8):
                    self.stt(yv[:, k, :], xv[:, k, :], fng[:, k:k + 1], rs, ALU.mult, ALU.mult, (x, rsb, self.b_fng), (yb,))
                self.store(self.yT.rearrange("(k p) t -> p k t", p=128)[:, :, t0:t0 + FB], yv, (yb,), (self.db("yT", i),))
            if i + 2 < NBF:
                load_x(i + 2)


def _pk(vec, k):
    return np.ascontiguousarray(np.asarray(vec, np.float32).reshape(k, 128).T)


def make_consts():
    ident = np.eye(128, dtype=np.float32)
    tri = (np.arange(128)[None, :] >= np.arange(128)[:, None]).astype(np.float32)
    perm = np.zeros((128, 128), np.float32)
    for r in range(128):
        rr = r % 64
        if rr < 8:
            perm[r + 8, r] = 1.0
        elif rr < 16:
            perm[r - 8, r] = 1.0
    cbf = np.concatenate([ident, tri, perm], 1).astype(ml_dtypes.bfloat16)
    inv_freq = 1.0 / (500000.0 ** (np.arange(0, 16, 2, dtype=np.float64) / 16))
    cf = np.zeros((128, 4), np.float32)
    for r in range(128):
        rr = r % 64
        if rr < 16:
            cf[r, 0] = np.float32(inv_freq[rr % 8])
            cf[r, 1] = -1.0 if rr < 8 else 1.0
    return cbf, cf


def prep_inputs(inp, T, L):
    f = lambda a: np.ascontiguousarray(np.asarray(a, np.float32))
    cbf, cf = make_consts()
    sh = {
        "cbf": cbf, "cf32": cf,
        "ada_w": f(inp["ada_w"])[:L],
        "ada_b": np.ascontiguousarray(np.stack([_pk(inp["ada_b"][l], 48) for l in range(L)], 1)),
        "n1g": np.ascontiguousarray(np.stack([_pk(inp["norm1_g"][l], 8) for l in range(L)], 1)),
        "n2g": np.ascontiguousarray(np.stack([_pk(inp["norm2_g"][l], 8) for l in range(L)], 1)),
        "fng": _pk(inp["final_g"], 8),
        "w_in": f(inp["w_in"])[:L],
        "lamv": np.ascontiguousarray(np.stack([np.stack([f(inp[n])[l] for n in ("lambda_q1", "lambda_k1", "lambda_q2", "lambda_k2")], 0)
                                               for l in range(L)], 0).reshape(1, -1)),
        "sublng": f(inp["subln_g"])[:L].reshape(1, -1),
        "w_ao": f(inp["w_attn_o"])[:L],
        "cw": np.ascontiguousarray(np.stack([f(inp["dw_conv_w"])[l][:, 0, :].reshape(CK, 4, 128).transpose(2, 1, 0) for l in range(L)], 1)),
        "cvec": np.ascontiguousarray(np.stack([np.stack([_pk(inp[n][l], 4) for n in ("dw_conv_b", "conv_ln_g", "conv_ln_b")], 1)
                                               for l in range(L)], 1)),
        "w_co": f(inp["w_conv_o"])[:L],
        "w_out": f(inp["w_out"])[:L],
        "w_f1": f(inp["w_ffn_in"])[:L],
        "w_f2": f(inp["w_ffn_out"])[:L],
    }
    return sh


def core_inputs(inp, sh, b, j, T, split):
    d = dict(sh)
    d["xT"] = np.ascontiguousarray(np.asarray(inp["x"][b], np.float32)[:T].T)
    d["cT"] = _pk(inp["c"][b], 8)
    pos = np.asarray(inp["positions"][b], np.int32)[:T]
    d["pos"] = np.ascontiguousarray(pos.reshape(1, T))
    if split:
        nb = T // TB
        order = [2 * s + j for s in range(nb // 2)] + [2 * s + 1 - j for s in range(nb // 2)]
        d["pos2"] = np.ascontiguousarray(pos.reshape(nb, TB)[order].reshape(1, T))
    else:
        d["pos2"] = d["pos"]
    cf = sh["cf32"].copy()
    cf[:, 2] = 1.0 - j
    cf[:, 3] = float(j)
    d["cf32"] = cf
    return d


def assemble(results, B, T, split):
    out = np.empty((B, T, D), np.float32)
    for b in range(B):
        if not split:
            out[b] = np.asarray(results[2 * b]["yT"], np.float32).T
        else:
            nb = T // TB
            for j in range(2):
                y = np.asarray(results[2 * b + j]["yT"], np.float32)
                for s in range(nb // 2):
                    g = 2 * s + j
                    out[b, g * TB:(g + 1) * TB] = y[:, s * TB:(s + 1) * TB].T
    return out


_CACHE = {}


def kernel(**inputs):
    B, T, _ = inputs["x"].shape
    L = DEPTH
    key = (T, L)
    if key not in _CACHE:
        bld = Builder(T, L)
        _CACHE[key] = (bld.build(), bld.split)
    nc, split = _CACHE[key]
    sh = prep_inputs(inputs, T, L)
    in_maps = [core_inputs(inputs, sh, c // 2, c % 2, T, split) for c in range(8)]
    res = run_bass_kernel_spmd(nc, in_maps, core_ids=list(range(8)))
    return assemble(res.results, B, T, split)
```
